# Optimizing a Trainium2 kernel written in Bass

```python
import math
import jax, jax.numpy as jnp
from jax import lax
import numpy as np

D_MODEL = 2048
BATCH = 8
SEQ = 2048
DEPTH = 1

HEAD_DIM = 128
GQA_Q_HEADS = 8
GQA_KV_HEADS = 2
NA_HEADS = 4
MEM_HEADS = 4
MEM_LEN = 256
D_FF = 5632
GRID_W = 64
NA_ROWS_MAX = 8
NA_COLS = 16
Q_BLOCK = 128
ROPE_THETA = 10000.0
AXIS_ROT_DIM = HEAD_DIM // 2
LN_EPS = 1e-5
RMS_EPS = 1e-6
N_BRANCHES = 3
DEEPNORM_ALPHA = (2 * DEPTH) ** 0.25
DEEPNORM_BETA = (8 * DEPTH) ** -0.25

WA_Q = GQA_Q_HEADS * HEAD_DIM
WA_KV = GQA_KV_HEADS * HEAD_DIM
WB = NA_HEADS * HEAD_DIM
WM = MEM_HEADS * HEAD_DIM
SPLITS = [WA_Q, WA_Q + WA_KV, WA_Q + 2 * WA_KV, WA_Q + 2 * WA_KV + 3 * WB,
          WA_Q + 2 * WA_KV + 3 * WB + WM]
W_IN_COLS = WA_Q + 2 * WA_KV + 3 * WB + WM + N_BRANCHES * D_MODEL

kernel_name = "hybrid_gqa_natten_memory_macaron_deepnorm"


def layer_norm(x, g, b):
    xf = x.astype(jnp.float32)
    mu = xf.mean(-1, keepdims=True)
    var = jnp.square(xf - mu).mean(-1, keepdims=True)
    y = (xf - mu) * lax.rsqrt(var + LN_EPS) * g.astype(jnp.float32) + b.astype(jnp.float32)
    return y.astype(x.dtype)


def head_rms_norm(x, g):
    xf = x.astype(jnp.float32)
    y = xf * lax.rsqrt(jnp.mean(xf * xf, axis=-1, keepdims=True) + RMS_EPS) * g.astype(jnp.float32)
    return y.astype(x.dtype)


def swiglu(x, w_gu, w_down):
    gate, up = jnp.split(x @ w_gu, 2, axis=-1)
    return (jax.nn.silu(gate) * up) @ w_down


def axial_tables(seq):
    t = jnp.arange(seq)
    row = (t // GRID_W).astype(jnp.float32)
    col = (t % GRID_W).astype(jnp.float32)
    inv = ROPE_THETA ** (-jnp.arange(0, AXIS_ROT_DIM, 2, dtype=jnp.float32) / AXIS_ROT_DIM)
    ang_r = row[:, None] * inv[None, :]
    ang_c = col[:, None] * inv[None, :]
    return jnp.cos(ang_r), jnp.sin(ang_r), jnp.cos(ang_c), jnp.sin(ang_c)


def axial_rope(x, cos_r, sin_r, cos_c, sin_c):
    xf = x.astype(jnp.float32)

    def rot(u, c, s):
        u1, u2 = jnp.split(u, 2, axis=-1)
        c = c[None, :, None, :]
        s = s[None, :, None, :]
        return jnp.concatenate([u1 * c - u2 * s, u2 * c + u1 * s], axis=-1)

    xr, xc = jnp.split(xf, 2, axis=-1)
    return jnp.concatenate([rot(xr, cos_r, sin_r), rot(xc, cos_c, sin_c)], axis=-1).astype(x.dtype)


def gqa_attention(q, k, v):
    b, s = q.shape[0], q.shape[1]
    g = GQA_Q_HEADS // GQA_KV_HEADS
    nb = s // Q_BLOCK
    qb = q.reshape(b, nb, Q_BLOCK, GQA_KV_HEADS, g, HEAD_DIM).transpose(1, 0, 3, 4, 2, 5)
    kt = k.transpose(0, 2, 1, 3)
    vt = v.transpose(0, 2, 1, 3)
    scale = HEAD_DIM ** -0.5

    def block(qi):
        sc = jnp.einsum('bhgqd,bhkd->bhgqk', qi, kt, preferred_element_type=jnp.float32) * scale
        p = jax.nn.softmax(sc, axis=-1).astype(vt.dtype)
        return jnp.einsum('bhgqk,bhkd->bhgqd', p, vt)

    o = lax.map(block, qb)
    return o.transpose(1, 0, 4, 2, 3, 5).reshape(b, s, GQA_Q_HEADS * HEAD_DIM)


def neighbourhood_attention(q, k, v, rpb):
    b, s, h, d = q.shape
    rows = s // GRID_W
    kr = min(NA_ROWS_MAX, rows)

    def to_grid(u):
        return u.reshape(b, rows, GRID_W, h, d).transpose(0, 3, 1, 2, 4)

    qg, kg, vg = to_grid(q), to_grid(k), to_grid(v)
    cols = np.arange(GRID_W)
    col_start = np.clip(cols - NA_COLS // 2, 0, GRID_W - NA_COLS)
    col_idx = col_start[:, None] + np.arange(NA_COLS)[None, :]
    dc_idx = col_idx - cols[:, None] + NA_COLS - 1
    scale = d ** -0.5

    def row_block(r):
        rs = jnp.clip(r - kr // 2, 0, rows - kr)
        kb = lax.dynamic_slice_in_dim(kg, rs, kr, axis=2)[:, :, :, col_idx, :]
        vb = lax.dynamic_slice_in_dim(vg, rs, kr, axis=2)[:, :, :, col_idx, :]
        qr = lax.dynamic_index_in_dim(qg, r, axis=2, keepdims=False)
        sc = jnp.einsum('bhqd,bhrqkd->bhqrk', qr, kb, preferred_element_type=jnp.float32) * scale
        dr_idx = rs + jnp.arange(kr) - r + NA_ROWS_MAX - 1
        bias = rpb[:, dr_idx[:, None, None], dc_idx[None, :, :]]
        sc = sc + bias.transpose(0, 2, 1, 3)[None].astype(jnp.float32)
        p = jax.nn.softmax(sc.reshape(b, h, GRID_W, kr * NA_COLS), axis=-1)
        p = p.reshape(sc.shape).astype(vb.dtype)
        return jnp.einsum('bhqrk,bhrqkd->bhqd', p, vb)

    o = lax.map(row_block, jnp.arange(rows))
    return o.transpose(1, 0, 3, 2, 4).reshape(b, s, h * d)


def memory_attention(q, km, vm):
    b, s = q.shape[0], q.shape[1]
    scale = HEAD_DIM ** -0.5
    sc = jnp.einsum('bqhd,bkhd->bhqk', q, km, preferred_element_type=jnp.float32) * scale
    p = jax.nn.softmax(sc, axis=-1).astype(vm.dtype)
    return jnp.einsum('bhqk,bkhd->bqhd', p, vm).reshape(b, s, MEM_HEADS * HEAD_DIM)


def hybrid_mixer(h, mem, w_in, b_gate, q_norm_a, k_norm_a, na_rpb, w_mem_kv,
                 w_oa, w_ob, w_om, w_out):
    b, s, _ = h.shape
    z = h @ w_in
    qa, ka, va, qkvb, qm, gates = jnp.split(z, SPLITS, axis=-1)
    qa = head_rms_norm(qa.reshape(b, s, GQA_Q_HEADS, HEAD_DIM), q_norm_a)
    ka = head_rms_norm(ka.reshape(b, s, GQA_KV_HEADS, HEAD_DIM), k_norm_a)
    va = va.reshape(b, s, GQA_KV_HEADS, HEAD_DIM)
    tabs = axial_tables(s)
    qa = axial_rope(qa, *tabs)
    ka = axial_rope(ka, *tabs)
    oa = gqa_attention(qa, ka, va)
    qkvb = qkvb.reshape(b, s, 3, NA_HEADS, HEAD_DIM)
    ob = neighbourhood_attention(qkvb[:, :, 0], qkvb[:, :, 1], qkvb[:, :, 2], na_rpb)
    kvm = (mem @ w_mem_kv).reshape(b, mem.shape[1], 2, MEM_HEADS, HEAD_DIM)
    om = memory_attention(qm.reshape(b, s, MEM_HEADS, HEAD_DIM), kvm[:, :, 0], kvm[:, :, 1])
    g = jax.nn.sigmoid(gates + b_gate).reshape(b, s, N_BRANCHES, D_MODEL)
    y = g[:, :, 0] * (oa @ w_oa) + g[:, :, 1] * (ob @ w_ob) + g[:, :, 2] * (om @ w_om)
    return y @ w_out


def setup_inputs(seed: int = 0) -> dict:
    key = jax.random.key(seed)
    ks = jax.random.split(key, 24)
    L = DEPTH
    beta = DEEPNORM_BETA

    def nrm(k, shape, std):
        return jax.random.normal(k, shape, dtype=jnp.float32) * std

    def gain(k, shape):
        return 1.0 + nrm(k, shape, 0.01)

    return {
        "x": nrm(ks[0], (BATCH, SEQ, D_MODEL), 1.0),
        "mem": nrm(ks[1], (BATCH, MEM_LEN, D_MODEL), 1.0),
        "ln1_g": gain(ks[2], (L, D_MODEL)),
        "ln1_b": nrm(ks[3], (L, D_MODEL), 0.01),
        "ffn1_w_gu": nrm(ks[4], (L, D_MODEL, 2 * D_FF), beta * D_MODEL ** -0.5),
        "ffn1_w_down": nrm(ks[5], (L, D_FF, D_MODEL), beta * D_FF ** -0.5),
        "w_in": nrm(ks[6], (L, D_MODEL, W_IN_COLS), D_MODEL ** -0.5),
        "b_gate": nrm(ks[7], (L, N_BRANCHES * D_MODEL), 0.01),
        "q_norm_a": gain(ks[8], (L, HEAD_DIM)),
        "k_norm_a": gain(ks[9], (L, HEAD_DIM)),
        "na_rpb": nrm(ks[10], (L, NA_HEADS, 2 * NA_ROWS_MAX - 1, 2 * NA_COLS - 1), 0.1),
        "w_mem_kv": nrm(ks[11], (L, D_MODEL, 2 * WM), D_MODEL ** -0.5),
        "w_oa": nrm(ks[12], (L, WA_Q, D_MODEL), beta * WA_Q ** -0.5),
        "w_ob": nrm(ks[13], (L, WB, D_MODEL), beta * WB ** -0.5),
        "w_om": nrm(ks[14], (L, WM, D_MODEL), beta * WM ** -0.5),
        "w_out": nrm(ks[15], (L, D_MODEL, D_MODEL), beta * D_MODEL ** -0.5),
        "ln2_g": gain(ks[16], (L, D_MODEL)),
        "ln2_b": nrm(ks[17], (L, D_MODEL), 0.01),
        "ffn2_w_gu": nrm(ks[18], (L, D_MODEL, 2 * D_FF), beta * D_MODEL ** -0.5),
        "ffn2_w_down": nrm(ks[19], (L, D_FF, D_MODEL), beta * D_FF ** -0.5),
        "ln3_g": gain(ks[20], (L, D_MODEL)),
        "ln3_b": nrm(ks[21], (L, D_MODEL), 0.01),
    }


def reference(x, mem, ln1_g, ln1_b, ffn1_w_gu, ffn1_w_down, w_in, b_gate, q_norm_a,
              k_norm_a, na_rpb, w_mem_kv, w_oa, w_ob, w_om, w_out, ln2_g, ln2_b,
              ffn2_w_gu, ffn2_w_down, ln3_g, ln3_b):
    alpha = DEEPNORM_ALPHA
    for l in range(DEPTH):
        x = layer_norm(alpha * x + 0.5 * swiglu(x, ffn1_w_gu[l], ffn1_w_down[l]), ln1_g[l], ln1_b[l])
        mix = hybrid_mixer(x, mem, w_in[l], b_gate[l], q_norm_a[l], k_norm_a[l], na_rpb[l],
                           w_mem_kv[l], w_oa[l], w_ob[l], w_om[l], w_out[l])
        x = layer_norm(alpha * x + mix, ln2_g[l], ln2_b[l])
        x = layer_norm(alpha * x + 0.5 * swiglu(x, ffn2_w_gu[l], ffn2_w_down[l]), ln3_g[l], ln3_b[l])
    return x
```

```python
import numpy as np
import concourse.bass as bass
import concourse.mybir as mybir
from concourse.bass_utils import run_bass_kernel_spmd

F32 = mybir.dt.float32
BF16 = mybir.dt.bfloat16
AF = mybir.ActivationFunctionType
ALU = mybir.AluOpType

S = 2048
D = 2048
DFF = 5632
NCORES = 8
ALPHA = float(2.0 ** 0.25)
LN_EPS = 1e-5
RMS_EPS = 1e-6
SCALE = float(128 ** -0.5)
NEG = -30000.0
SKIP_FFN1 = False
STOP_AFTER = None

ITEM = {F32: 4, BF16: 2}


class Op:
    __slots__ = ("eng", "fn", "deps", "signal", "seq", "is_dma", "dsem", "dval", "idx")

    def __init__(self, eng, fn, is_dma):
        self.eng = eng
        self.fn = fn
        self.deps = []
        self.signal = False
        self.seq = 0
        self.is_dma = is_dma
        self.dsem = None
        self.dval = 0


GRAN = 512


def ap_keys(ap):
    tname = type(ap.tensor).__name__
    if tname.startswith("SB"):
        space = "s"
    elif tname.startswith("PSum"):
        space = "p"
    else:
        return []
    isz = ITEM[ap.dtype]
    pat = ap.ap
    pstep = pat[0][0]
    off = ap.offset % pstep if pstep > 0 else ap.offset
    if space == "p":
        return {("p", (off * isz) // 2048)}
    dims = [(s, c) for (s, c) in pat[1:] if c > 1]
    if not dims:
        dims = [(1, 1)]
    inner_s, inner_c = dims[-1]
    outer = dims[:-1]
    n_outer = 1
    for (_, c) in outer:
        n_outer *= c
    keys = set()
    if n_outer > 256:
        hi = off + sum((c - 1) * s for (s, c) in dims) + 1
        for g in range((off * isz) // GRAN, (hi * isz - 1) // GRAN + 1):
            keys.add((space, g))
        return keys
    offs = [off]
    for (s, c) in outer:
        offs = [o + i * s for o in offs for i in range(c)]
    ext = (inner_c - 1) * inner_s + 1
    for o in offs:
        for g in range((o * isz) // GRAN, ((o + ext) * isz - 1) // GRAN + 1):
            keys.add((space, g))
    return keys


class Prog:
    ENGS = ("pe", "act", "dve", "pool", "sp")
    NDSEM = 24

    def __init__(self, nc):
        self.nc = nc
        self.ops = {e: [] for e in self.ENGS}
        self.last_writer = {}
        self.readers = {}
        self.dma_count = {"pool": 0, "sp": 0, "act": 0}
        self.dma_last = {}
        self.n = 0

    def _keys(self, items):
        ks = set()
        for it in items:
            if isinstance(it, tuple):
                ks.add(it)
            else:
                ks.update(ap_keys(it))
        return ks

    def add(self, eng, fn, reads=(), writes=(), dma=False):
        op = Op(eng, fn, dma)
        op.idx = self.n
        self.n += 1
        rk = self._keys(reads)
        wk = self._keys(writes)
        deps = {}
        war = {}
        for k in rk:
            w = self.last_writer.get(k)
            if w is not None:
                deps[id(w)] = w
        for k in wk:
            w = self.last_writer.get(k)
            if w is not None:
                deps[id(w)] = w
            for r in self.readers.get(k, {}).values():
                war[id(r)] = r
        if dma:
            q = eng
            n = self.dma_count[q]
            self.dma_count[q] = n + 1
            slot = (q, n % self.NDSEM)
            prev = self.dma_last.get(slot)
            if prev is not None:
                deps[id(prev)] = prev
                op.dval = prev.dval + 16
            else:
                op.dval = 16
            op.dsem = slot
            self.dma_last[slot] = op
        for d in deps.values():
            if d is op:
                continue
            if d.is_dma:
                op.deps.append(d)
            elif d.eng != eng or dma or eng != "pe":
                d.signal = True
                op.deps.append(d)
        for d in war.values():
            if d is op or id(d) in deps:
                continue
            if d.is_dma:
                op.deps.append(d)
            elif d.eng != eng or dma:
                d.signal = True
                op.deps.append(d)
        for k in wk:
            self.last_writer[k] = op
            self.readers[k] = {}
        for k in rk:
            if k in wk:
                continue
            self.readers.setdefault(k, {})[eng + ("_d%d" % (op.dsem[1]) if dma else "")] = op
        self.ops[eng].append(op)
        return op

    def emit(self, handles, esem, dsems):
        for e in self.ENGS:
            c = 0
            for op in self.ops[e]:
                if op.signal and not op.is_dma:
                    c += 1
                op.seq = c

        def run(e):
            h = handles[e]
            waited = {}
            for op in self.ops[e]:
                need = {}
                for d in op.deps:
                    if d.is_dma:
                        key = ("d",) + d.dsem
                        sem = dsems[d.dsem]
                        val = d.dval
                    else:
                        key = ("e", d.eng)
                        sem = esem[d.eng]
                        val = d.seq
                    if need.get(key, (None, 0))[1] < val:
                        need[key] = (sem, val)
                for key, (sem, val) in need.items():
                    if waited.get(key, 0) < val:
                        h.wait_ge(sem, val)
                        waited[key] = val
                ins = op.fn(h)
                if op.is_dma:
                    ins.then_inc(dsems[op.dsem], 16)
                elif op.signal:
                    ins.then_inc(esem[e], 1)
        return run


class Builder:
    def __init__(self):
        self.nc = bass.Bass("TRN2", target_bir_lowering=False)
        self.P = Prog(self.nc)
        self.bank_ctr = 0
        self.lo_ctr = 0
        self.hi_ctr = 0
        self.ring_ctr = 0

    def din(self, name, shape, dt=F32):
        return self.nc.dram_tensor(name, list(shape), dt, kind="ExternalInput").ap()

    def dscr(self, name, shape, dt=F32):
        return self.nc.dram_tensor(name, list(shape), dt, kind="Internal").ap()

    def sb(self, off, shape, dt):
        n = 1
        for s in shape[1:]:
            n *= s
        nb = n * ITEM[dt]
        assert off % 4 == 0 and off + nb <= self.arena_bytes, (off, nb)
        v = self.arena[:, off // 2: (off + nb) // 2]
        if dt == F32:
            v = v.bitcast(F32)
        if len(shape) == 3:
            v = v.rearrange("p (a b) -> p a b", a=shape[1])
        elif len(shape) == 4:
            v = v.rearrange("p (a b c) -> p a b c", a=shape[1], b=shape[2])
        if shape[0] != 128:
            v = v[0:shape[0]]
        return v

    def bank(self, dt=F32):
        b = self.bank_ctr % 8
        self.bank_ctr += 1
        v = self.psum[:, b, :]
        if dt == BF16:
            v = v.bitcast(BF16)
        return v

    def bank_lo(self, dt=F32):
        b = self.lo_ctr % 4
        self.lo_ctr += 1
        v = self.psum[:, b, :]
        if dt == BF16:
            v = v.bitcast(BF16)
        return v

    def bank_hi(self):
        b = 4 + self.hi_ctr % 4
        self.hi_ctr += 1
        return self.psum[:, b, :]

    def mm1(self, out, lhsT, rhs, start, stop):
        return self.P.add("pe", lambda h: h.matmul(out, lhsT, rhs, start=start, stop=stop),
                          reads=[lhsT, rhs], writes=[out])

    def recip(self, out, in_):
        return self.P.add("dve", lambda h: h.reciprocal(out, in_), reads=[in_], writes=[out])

    def dma(self, q, out, in_, reads=(), writes=()):
        return self.P.add(q, lambda h, o=out, i=in_: h.dma_start(out=o, in_=i),
                          reads=list(reads) + [in_], writes=list(writes) + [out], dma=True)

    def mm(self, out, pairs, extra_reads=()):
        def fn(h, out=out, pairs=pairs):
            n = len(pairs)
            ins = None
            for i, (l, r) in enumerate(pairs):
                ins = h.matmul(out, l, r, start=(i == 0), stop=(i == n - 1))
            return ins
        reads = []
        for (l, r) in pairs:
            reads.append(l)
            reads.append(r)
        return self.P.add("pe", fn, reads=reads + list(extra_reads), writes=[out])

    def tr(self, out, in_, ident):
        return self.P.add("pe", lambda h: h.transpose(out, in_, ident), reads=[in_, ident], writes=[out])

    def act(self, out, in_, func, bias=None, scale=None, accum_out=None, extra_reads=()):
        kw = {}
        reads = [in_] + list(extra_reads)
        if bias is not None:
            kw["bias"] = bias
            if not isinstance(bias, (int, float)):
                reads.append(bias)
        if scale is not None:
            kw["scale"] = scale
            if not isinstance(scale, (int, float)):
                reads.append(scale)
        writes = [out]
        if accum_out is not None:
            kw["accum_out"] = accum_out
            writes.append(accum_out)
        return self.P.add("act", lambda h: h.activation(out, in_, func, **kw), reads=reads, writes=writes)

    def tt(self, eng, out, in0, in1, op):
        return self.P.add(eng, lambda h: h.tensor_tensor(out, in0, in1, op), reads=[in0, in1], writes=[out])

    def stt(self, out, in0, scalar, in1, op0, op1):
        reads = [in0, in1]
        if not isinstance(scalar, (int, float)):
            reads.append(scalar)
        return self.P.add("dve", lambda h: h.scalar_tensor_tensor(out, in0, scalar, in1, op0, op1),
                          reads=reads, writes=[out])

    def ts(self, eng, out, in0, s1, s2, op0, op1=None):
        reads = [in0]
        for s in (s1, s2):
            if s is not None and not isinstance(s, (int, float)):
                reads.append(s)
        if op1 is None:
            return self.P.add(eng, lambda h: h.tensor_scalar(out, in0, s1, None, op0), reads=reads, writes=[out])
        return self.P.add(eng, lambda h: h.tensor_scalar(out, in0, s1, s2, op0, op1), reads=reads, writes=[out])

    def copy(self, eng, out, in_):
        if eng == "act":
            return self.P.add("act", lambda h: h.copy(out, in_), reads=[in_], writes=[out])
        return self.P.add(eng, lambda h: h.tensor_copy(out, in_), reads=[in_], writes=[out])

    def ring_slot(self):
        s = self.ring_ctr % 4
        self.ring_ctr += 1
        return self.RING + s * 8192

    def build(self):
        nc = self.nc
        B = self
        x = B.din("x", [S, D])
        mem = B.din("mem", [256, D])
        ln_g = [B.din("ln%d_g" % i, [1, D]) for i in (1, 2, 3)]
        ln_b = [B.din("ln%d_b" % i, [1, D]) for i in (1, 2, 3)]
        w_gu = [B.din("ffn%d_w_gu" % i, [D, 2 * DFF]) for i in (1, 2)]
        w_dn = [B.din("ffn%d_w_down" % i, [DFF, D]) for i in (1, 2)]
        w_in = B.din("w_in", [D, 9728])
        b_gate = B.din("b_gate", [128, 48])
        qk_gain = B.din("qk_gain", [1, 640])
        nab = B.din("nab", [4, 128, 2048])
        w_mem_kv = B.din("w_mem_kv", [D, 1024])
        w_oa = B.din("w_oa", [1024, D])
        w_ob = B.din("w_ob", [512, D])
        w_om = B.din("w_om", [512, D])
        w_out = B.din("w_out", [D, D])
        ident_d = B.din("ident", [128, 128])
        rope_d = B.din("rope", [S, 256])
        out = nc.dram_tensor("out", [S, D], F32, kind="ExternalOutput").ap()
        r_scr = B.dscr("r_scr", [S, D])
        x1_scr = B.dscr("x1_scr", [S, D])
        x2_scr = B.dscr("x2_scr", [S, D])
        yT_scr = B.dscr("yT_scr", [16, 128, S], BF16)
        self.dbg = None

        self.arena_bytes = 204 * 1024
        XT_OFF = 0
        ACTV = 65536
        self.RING = 131072
        WORK = 163840
        CONST = 196608
        GB = ACTV + 49152

        with (
            nc.sbuf_tensor("arena", [128, self.arena_bytes // 2], BF16) as arena,
            nc.psum_tensor("psum", [128, 8, 512], F32) as psum,
            nc.semaphore("s_pe") as s_pe,
            nc.semaphore("s_act") as s_act,
            nc.semaphore("s_dve") as s_dve,
            nc.semaphore("s_pool") as s_pool,
            nc.semaphore("s_sp") as s_sp,
        ):
            self.arena = arena
            self.psum = psum
            XT = B.sb(XT_OFF, [128, 16, S], BF16)
            ident = B.sb(CONST, [128, 128], F32)
            identb = B.sb(CONST + 512, [128, 128], BF16)
            onesb = B.sb(CONST + 768, [128, 128], BF16)
            stats = B.sb(CONST + 1024, [128, 16, 4, 6], F32)
            mv = B.sb(CONST + 2560, [128, 16, 2], F32)
            smal = B.sb(CONST + 2688, [128, 16, 4], F32)
            bgate = B.sb(CONST + 2944, [128, 48], F32)
            gains = B.sb(CONST + 3136, [128, 640], F32)
            epsT = B.sb(CONST + 9792, [128, 1], F32)
            KM_OFF = CONST + 5696
            VM_OFF = KM_OFF + 2048
            assert VM_OFF + 2048 <= self.arena_bytes

            B.dma("sp", ident, ident_d)
            B.copy("dve", identb, ident)
            B.P.add("pool", lambda h: h.memset(onesb, 1.0), writes=[onesb])
            B.P.add("pool", lambda h: h.memset(epsT, LN_EPS), writes=[epsT])

            def transpose_tile(R, m, evac=("act", "dve")):
                for q in range(4):
                    pb = B.bank()
                    for kk in range(4):
                        k = q * 4 + kk
                        B.tr(pb[:, kk * 128:(kk + 1) * 128], R[:, k * 128:(k + 1) * 128], ident)
                    dst = XT[:, q * 4:(q + 1) * 4, m * 128:(m + 1) * 128]
                    src = pb.rearrange("p (a b) -> p a b", a=4)
                    B.copy(evac[q % 2], dst, src)

            def load_x_to_XT(src_dram, mid_fn=None):
                for m in range(16):
                    if m == 8 and mid_fn is not None:
                        mid_fn()
                    R = B.sb(ACTV + (m % 4) * 8192, [128, D], F32)
                    B.dma("sp", R, src_dram[m * 128:(m + 1) * 128, :])
                    transpose_tile(R, m)

            def load_gb(i):
                g = B.sb(GB, [128, D], F32)
                b = B.sb(GB + 8192, [128, D], F32)
                B.dma("sp", g, ln_g[i].broadcast_to([128, D]))
                B.dma("sp", b, ln_b[i].broadcast_to([128, D]))
                return g, b

            def ln_pass(src_scr, dst_dram, g, b, dst_key, do_transpose):
                def Rt(m):
                    return B.sb(ACTV + (m % 6) * 8192, [128, D], F32)

                def ld(m):
                    B.dma("sp", Rt(m), src_scr[m * 128:(m + 1) * 128, :], reads=[("r", m, n) for n in range(4)])

                def st_stage(m):
                    mvm = mv[:, m, :]
                    B.P.add("dve", lambda h, o=mvm, i=stats[:, m].rearrange("p a b -> p (a b)"): h.bn_aggr(o, i),
                            reads=[stats[:, m]], writes=[mvm])
                    sd = smal[:, m, 0:1]
                    B.ts("dve", sd, mv[:, m, 1:2], LN_EPS, None, ALU.add)
                    B.act(sd, sd, AF.Sqrt)

                def nrm_stage(m):
                    R = Rt(m)
                    sd = smal[:, m, 0:1]
                    rstd = smal[:, m, 1:2]
                    B.recip(rstd, sd)
                    B.stt(R, R, mv[:, m, 0:1], g, ALU.subtract, ALU.mult)
                    B.stt(R, R, rstd, b, ALU.mult, ALU.add)
                    B.dma("sp", dst_dram[m * 128:(m + 1) * 128, :], R, writes=[(dst_key, m)])
                    if do_transpose:
                        transpose_tile(R, m, evac=("act", "act"))

                for m in range(5):
                    ld(m)
                st_stage(0)
                st_stage(1)
                for m in range(16):
                    if m + 5 < 16:
                        ld(m + 5)
                    if m + 2 < 16:
                        st_stage(m + 2)
                    nrm_stage(m)

            EP_LOOK = 10
            EP_RING = 14

            def ep_load(res_dram, res_keys, m, n, piece_idx):
                rp = B.sb(WORK + (piece_idx % EP_RING) * 2048, [128, 512], F32)
                B.dma("sp", rp, res_dram[m * 128:(m + 1) * 128, n * 512:(n + 1) * 512], reads=res_keys)

            def ep_finish(pb, scale, m, n, last, out_scr, piece_idx):
                rp = B.sb(WORK + (piece_idx % EP_RING) * 2048, [128, 512], F32)
                if scale is None:
                    B.tt("dve", rp, rp, pb, ALU.add)
                else:
                    B.stt(rp, rp, scale, pb, ALU.mult, ALU.add)
                if last:
                    B.P.add("dve", lambda h, o=stats[:, m, n, :], i=rp: h.bn_stats(o, i),
                            reads=[rp], writes=[stats[:, m, n, :]])
                B.dma("sp", out_scr[m * 128:(m + 1) * 128, n * 512:(n + 1) * 512], rp, writes=[("r", m, n)])

            def proj_phase(mm_fn, wload_fn, res_dram, res_key_fn, scale, last, piece0):
                order = [(n, m) for n in range(4) for m in range(16)]
                for i in range(min(EP_LOOK, len(order))):
                    n, m = order[i]
                    ep_load(res_dram, res_key_fn(m, n), m, n, piece0 + i)
                for i, (n, m) in enumerate(order):
                    if m == 0:
                        wload_fn(n)
                    if i + EP_LOOK < len(order):
                        n2, m2 = order[i + EP_LOOK]
                        ep_load(res_dram, res_key_fn(m2, n2), m2, n2, piece0 + i + EP_LOOK)
                    pb = B.bank()
                    mm_fn(pb, n, m)
                    ep_finish(pb, scale, m, n, last, r_scr, piece0 + i)
                return piece0 + len(order)

            def ffn(fi, res_dram, res_key, dst_dram, dst_key, ln_i, do_transpose):
                wgu = w_gu[fi].rearrange("(k p) n -> p k n", p=128)
                wdn = w_dn[fi]
                g, b = load_gb(ln_i)
                parts = [6, 6, 6, 4]
                tile0 = 0
                piece = 0
                for pi, ntile in enumerate(parts):
                    nch = ntile * 2
                    HT = B.sb(ACTV, [128, 12, S], BF16)
                    for ti in range(ntile):
                        c0 = (tile0 + ti) * 256
                        Wg = B.sb(B.ring_slot(), [128, 16, 256], BF16)
                        Wu = B.sb(B.ring_slot(), [128, 16, 256], BF16)
                        B.dma("pool", Wg, wgu[:, :, c0:c0 + 256])
                        B.dma("pool", Wu, wgu[:, :, DFF + c0:DFF + c0 + 256])
                        for jj in range(2):
                            j = ti * 2 + jj
                            for t in range(4):
                                pg = B.bank()
                                pu = B.bank()
                                B.mm(pg, [(Wg[:, k, jj * 128:(jj + 1) * 128], XT[:, k, t * 512:(t + 1) * 512]) for k in range(16)])
                                B.mm(pu, [(Wu[:, k, jj * 128:(jj + 1) * 128], XT[:, k, t * 512:(t + 1) * 512]) for k in range(16)])
                                sg = B.sb(WORK + 28672 + ((j * 4 + t) % 2) * 2048, [128, 512], F32)
                                B.act(sg, pg, AF.Silu)
                                B.stt(HT[:, j, t * 512:(t + 1) * 512], sg, 0.5, pu, ALU.mult, ALU.mult)
                    row0 = tile0 * 256
                    last = (pi == len(parts) - 1)
                    wts = {}

                    def wload(n, row0=row0, nch=nch, wts=wts):
                        WA = B.sb(B.ring_slot(), [128, 8, 512], BF16)
                        WB = B.sb(B.ring_slot(), [128, 8, 512], BF16)
                        B.dma("pool", WA, wdn[row0:row0 + 1024, n * 512:(n + 1) * 512].rearrange("(j p) n -> p j n", p=128))
                        B.dma("pool", WB[:, 0:nch - 8, :], wdn[row0 + 1024:row0 + nch * 128, n * 512:(n + 1) * 512].rearrange("(j p) n -> p j n", p=128))
                        wts[n] = (WA, WB)

                    def mmf(pb, n, m, nch=nch, wts=wts, HT=HT):
                        WA, WB = wts[n]
                        pairs = [(HT[:, j, m * 128:(m + 1) * 128], WA[:, j, :]) for j in range(8)]
                        pairs += [(HT[:, 8 + j, m * 128:(m + 1) * 128], WB[:, j, :]) for j in range(nch - 8)]
                        B.mm(pb, pairs)

                    if last:
                        fused_last_part(HT, wdn, row0, dst_dram, dst_key, g, b, do_transpose)
                    elif pi == 0:
                        piece = proj_phase(mmf, wload, res_dram, lambda m, n: [(res_key, m)], ALPHA, False, piece)
                    else:
                        piece = proj_phase(mmf, wload, r_scr, lambda m, n: [("r", m, n)], None, False, piece)
                    tile0 += ntile

            def fused_last_part(HT, wdn, row0, dst_dram, dst_key, g, b, do_transpose):
                Ws = []
                for n in range(4):
                    W = B.sb(B.ring_slot(), [128, 8, 512], BF16)
                    B.dma("pool", W, wdn[row0:row0 + 1024, n * 512:(n + 1) * 512].rearrange("(j p) n -> p j n", p=128))
                    Ws.append(W)
                RO = [ACTV + 32768, ACTV + 40960, WORK, WORK + 8192, WORK + 20480]

                def Rt(m):
                    return B.sb(RO[m % 5], [128, D], F32)

                def ld(m):
                    B.dma("sp", Rt(m), r_scr[m * 128:(m + 1) * 128, :], reads=[("r", m, n) for n in range(4)])

                def stage_a(m):
                    R = Rt(m)
                    sm = stats[:, m].rearrange("p a b -> p (a b)")
                    pbs = []
                    for n in range(4):
                        pb = B.bank()
                        B.mm(pb, [(HT[:, j, m * 128:(m + 1) * 128], Ws[n][:, j, :]) for j in range(8)])
                        pbs.append(pb)
                    for n in range(4):
                        Rn = R[:, n * 512:(n + 1) * 512]
                        B.P.add("dve", lambda h, o=Rn, p_=pbs[n], a=sm[:, n:n + 1]:
                                h.scalar_tensor_tensor(o, p_, 1.0, o, ALU.mult, ALU.add, accum_out=a),
                                reads=[pbs[n], Rn], writes=[Rn, sm[:, n:n + 1]])
                    junk = B.sb(WORK + 16384, [128, D], BF16)
                    B.act(junk, R, AF.Square, accum_out=sm[:, 9:10])

                def stage_b(m):
                    R = Rt(m)
                    sm = stats[:, m].rearrange("p a b -> p (a b)")
                    t = sm[:, 8:9]
                    mean = sm[:, 10:11]
                    ex2 = sm[:, 11:12]
                    nvar = sm[:, 12:13]
                    ve = sm[:, 13:14]
                    rstd = sm[:, 14:15]
                    B.P.add("dve", lambda h: h.reduce_sum(t, sm[:, 0:4], mybir.AxisListType.X), reads=[sm[:, 0:4]], writes=[t])
                    B.ts("dve", sm[:, 10:12], sm[:, 8:10], 1.0 / D, None, ALU.mult)
                    B.stt(nvar, mean, mean, ex2, ALU.mult, ALU.subtract)
                    B.act(ve, nvar, AF.Sqrt, bias=epsT, scale=-1.0)
                    B.stt(R, R, mean, g, ALU.subtract, ALU.mult)
                    B.recip(rstd, ve)
                    B.act(R, R, AF.Copy, scale=rstd)
                    B.tt("pool", R, R, b, ALU.add)
                    B.dma("sp", dst_dram[m * 128:(m + 1) * 128, :], R, writes=[(dst_key, m)])

                ld(0)
                ld(1)
                for m in range(20):
                    if m >= 3 and m - 3 < 16 and do_transpose:
                        transpose_tile(Rt(m - 3), m - 3, evac=("act", "act"))
                    if m + 2 < 16:
                        ld(m + 2)
                    if m < 16:
                        stage_a(m)
                    if 1 <= m <= 16:
                        stage_b(m - 1)

            def mem_prep():
                kmT = B.sb(KM_OFF, [128, 4, 256], BF16)
                vm = B.sb(VM_OFF, [128, 2, 512], BF16)
                memT = B.sb(ACTV + 49152, [128, 16, 256], BF16)
                for i in range(2):
                    R = B.sb(ACTV + 32768 + i * 8192, [128, D], F32)
                    B.dma("sp", R, mem[i * 128:(i + 1) * 128, :])
                    for q in range(4):
                        pb = B.bank()
                        for kk in range(4):
                            k = q * 4 + kk
                            B.tr(pb[:, kk * 128:(kk + 1) * 128], R[:, k * 128:(k + 1) * 128], ident)
                        B.copy("act" if q % 2 == 0 else "dve", memT[:, q * 4:(q + 1) * 4, i * 128:(i + 1) * 128],
                               pb.rearrange("p (a b) -> p a b", a=4))
                wmv = w_mem_kv.rearrange("(k p) n -> p k n", p=128)
                for ti in range(4):
                    W = B.sb(B.ring_slot(), [128, 16, 256], BF16)
                    B.dma("pool", W, wmv[:, :, ti * 256:(ti + 1) * 256])
                    if ti < 2:
                        for hh in range(2):
                            h_ = ti * 2 + hh
                            pb = B.bank()
                            B.mm(pb[:, 0:256], [(W[:, k, hh * 128:(hh + 1) * 128], memT[:, k, :]) for k in range(16)])
                            B.copy("act", kmT[:, h_, :], pb[:, 0:256])
                    else:
                        c0 = (ti - 2) * 256
                        for kt in range(2):
                            pb = B.bank()
                            B.mm(pb[:, 0:256], [(memT[:, k, kt * 128:(kt + 1) * 128], W[:, k, :]) for k in range(16)])
                            B.copy("dve", vm[:, kt, c0:c0 + 256], pb[:, 0:256])


            def attention_stream(blocks, pt_base, rden_off, look=2):
                steps = []
                for bi, blk in enumerate(blocks):
                    for kt in range(len(blk["k"])):
                        steps.append((bi, kt))
                state = {}
                pts = {}
                cnt = [0]

                def qk(si):
                    bi, kt = steps[si]
                    blk = blocks[bi]
                    ps = B.bank_lo()
                    B.mm1(ps, blk["k"][kt], blk["q"], True, True)
                    Pt = B.sb(pt_base + (cnt[0] % 4) * 1024, [128, 512], BF16)
                    cnt[0] += 1
                    B.act(Pt, ps, AF.Exp, scale=SCALE)
                    pts[si] = Pt

                for si in range(min(look, len(steps))):
                    qk(si)
                for si, (bi, kt) in enumerate(steps):
                    blk = blocks[bi]
                    nk = len(blk["k"])
                    if si + look < len(steps):
                        qk(si + look)
                    if kt == 0:
                        state[bi] = (B.bank_hi(), B.bank_hi())
                    po, pd = state[bi]
                    Pt = pts.pop(si)
                    B.mm1(po, blk["v"][kt], Pt, kt == 0, kt == nk - 1)
                    B.mm1(pd, onesb, Pt, kt == 0, kt == nk - 1)
                    if kt == nk - 1:
                        rden = B.sb(rden_off + (bi % 2) * 2048, [128, 512], F32)
                        B.recip(rden, pd)
                        B.tt("dve", blk["dst"], po, rden, ALU.mult)
                        del state[bi]

            def mixer(stop=None):
                winv = w_in.rearrange("(k p) n -> p k n", p=128)
                oaT = B.sb(ACTV, [128, 8, S], BF16)
                obT = B.sb(ACTV + 32768, [128, 4, S], BF16)
                omT = B.sb(ACTV + 49152, [128, 4, S], BF16)
                kmT = B.sb(KM_OFF, [128, 4, 256], BF16)
                vm = B.sb(VM_OFF, [128, 2, 512], BF16)
                B.dma("sp", bgate, b_gate)
                B.dma("sp", gains, qk_gain.broadcast_to([128, 640]))
                XTm = XT

                if stop == 'm_prep':
                    return
                TA = ACTV + 32768
                for g_ in range(2):
                    qaT = B.sb(WORK, [128, 4, S], BF16)
                    kaT = B.sb(WORK + 16384, [128, S], BF16)
                    va = B.sb(WORK + 20480, [128, 16, 128], BF16)
                    Wq0 = B.sb(B.ring_slot(), [128, 16, 256], BF16)
                    Wq1 = B.sb(B.ring_slot(), [128, 16, 256], BF16)
                    Wkv = B.sb(B.ring_slot(), [128, 16, 256], BF16)
                    B.dma("pool", Wq0, winv[:, :, g_ * 512:g_ * 512 + 256])
                    B.dma("pool", Wq1, winv[:, :, g_ * 512 + 256:g_ * 512 + 512])
                    B.dma("pool", Wkv[:, :, 0:128], winv[:, :, 1024 + g_ * 128:1024 + (g_ + 1) * 128])
                    B.dma("pool", Wkv[:, :, 128:256], winv[:, :, 1280 + g_ * 128:1280 + (g_ + 1) * 128])
                    def a_stage1(m):
                        xs = lambda k: XTm[:, k, m * 128:(m + 1) * 128]
                        pq = B.bank()
                        B.mm(pq[:, 0:256], [(xs(k), Wq0[:, k, :]) for k in range(16)])
                        B.mm(pq[:, 256:512], [(xs(k), Wq1[:, k, :]) for k in range(16)])
                        pk = B.bank()
                        B.mm(pk[:, 0:256], [(xs(k), Wkv[:, k, :]) for k in range(16)])
                        base = TA + (m % 3) * 10240
                        zt = B.sb(base, [128, 640], F32)
                        t1 = B.sb(base + 2560, [128, 640], F32)
                        t2 = B.sb(base + 5120, [128, 640], F32)
                        qn = B.sb(base + 7680, [128, 640], BF16)
                        ropeT = B.sb(base + 8960, [128, 256], F32)
                        sm = B.sb(base + 9984, [128, 16], F32)
                        B.dma("sp", ropeT, rope_d[m * 128:(m + 1) * 128, :])
                        ssum = sm[:, 0:5]
                        rs = sm[:, 8:13]
                        for hh in range(4):
                            B.act(t1[:, hh * 128:(hh + 1) * 128], pq[:, hh * 128:(hh + 1) * 128], AF.Square, accum_out=ssum[:, hh:hh + 1])
                        B.act(t1[:, 512:640], pk[:, 0:128], AF.Square, accum_out=ssum[:, 4:5])
                        B.copy("act", zt[:, 0:512], pq)
                        B.copy("act", zt[:, 512:640], pk[:, 0:128])
                        B.copy("act", va[:, m, :], pk[:, 128:256])
                        B.ts("dve", ssum, ssum, 1.0 / 128.0, RMS_EPS, ALU.mult, ALU.add)
                        B.act(ssum, ssum, AF.Sqrt)
                        B.recip(rs, ssum)
                        z3 = zt.rearrange("p (a b) -> p a b", a=5)
                        B.tt("dve", zt, zt, gains, ALU.mult)
                        C3 = ropeT[:, 0:128].unsqueeze(1).broadcast_to([128, 5, 128])
                        B.tt("dve", t1.rearrange("p (a b) -> p a b", a=5), z3, C3, ALU.mult)
                        t23 = t2.rearrange("p (a b) -> p a b", a=5)
                        for blk in range(2):
                            for half in range(2):
                                o_ = blk * 64 + half * 32
                                i_ = blk * 64 + (1 - half) * 32
                                Sg = ropeT[:, 128 + o_:128 + o_ + 32].unsqueeze(1).broadcast_to([128, 5, 32])
                                B.tt("pool" if half else "dve", t23[:, :, o_:o_ + 32], z3[:, :, i_:i_ + 32], Sg, ALU.mult)
                        B.tt("pool", t1, t1, t2, ALU.add)
                        B.tt("dve", qn.rearrange("p (a b) -> p a b", a=5), t1.rearrange("p (a b) -> p a b", a=5),
                             rs.unsqueeze(2).broadcast_to([128, 5, 128]), ALU.mult)

                    def a_stage2(m):
                        qn = B.sb(TA + (m % 3) * 10240 + 7680, [128, 640], BF16)
                        pt = B.bank(BF16)
                        for hh in range(5):
                            B.tr(pt[:, hh * 128:(hh + 1) * 128], qn[:, hh * 128:(hh + 1) * 128], identb)
                        B.copy("act", qaT[:, :, m * 128:(m + 1) * 128], pt[:, 0:512].rearrange("p (a b) -> p a b", a=4))
                        B.copy("act", kaT[:, m * 128:(m + 1) * 128], pt[:, 512:640])

                    for m in range(18):
                        if m < 16:
                            a_stage1(m)
                        if m >= 2:
                            a_stage2(m - 2)
                    if stop == 'm_Aproj':
                        return
                    blocks = []
                    for hh in range(4):
                        for qb in range(4):
                            blocks.append(dict(q=qaT[:, hh, qb * 512:(qb + 1) * 512],
                                               k=[kaT[:, kt * 128:(kt + 1) * 128] for kt in range(16)],
                                               v=[va[:, kt, :] for kt in range(16)],
                                               dst=oaT[:, g_ * 4 + hh, qb * 512:(qb + 1) * 512]))
                    attention_stream(blocks, WORK + 24576, WORK + 28672)

                if stop == 'm_A':
                    return
                for h_ in range(4):
                    qbT = B.sb(WORK, [128, S], BF16)
                    kbT = B.sb(WORK + 4096, [128, S], BF16)
                    vb = B.sb(WORK + 8192, [128, 16, 128], BF16)
                    vbs = B.sb(WORK + 12288, [128, 15, 128], BF16)
                    nabT = B.sb(WORK + 16384, [128, 8, 256], F32)
                    B.dma("sp", nabT, nab[h_].rearrange("p (a b) -> p a b", a=8))
                    Wqk = B.sb(B.ring_slot(), [128, 16, 256], BF16)
                    Wv = B.sb(B.ring_slot(), [128, 16, 256], BF16)
                    B.dma("pool", Wqk[:, :, 0:128], winv[:, :, 1536 + h_ * 128:1536 + (h_ + 1) * 128])
                    B.dma("pool", Wqk[:, :, 128:256], winv[:, :, 2048 + h_ * 128:2048 + (h_ + 1) * 128])
                    B.dma("pool", Wv[:, :, 0:128], winv[:, :, 2560 + h_ * 128:2560 + (h_ + 1) * 128])
                    for t in range(4):
                        pb = B.bank()
                        B.mm(pb, [(Wqk[:, k, 0:128], XTm[:, k, t * 512:(t + 1) * 512]) for k in range(16)])
                        B.copy("act", qbT[:, t * 512:(t + 1) * 512], pb)
                        pb = B.bank()
                        B.mm(pb, [(Wqk[:, k, 128:256], XTm[:, k, t * 512:(t + 1) * 512]) for k in range(16)])
                        B.copy("act", kbT[:, t * 512:(t + 1) * 512], pb)
                    for m4 in range(4):
                        pb = B.bank()
                        for mm_ in range(4):
                            m = m4 * 4 + mm_
                            B.mm(pb[:, mm_ * 128:(mm_ + 1) * 128], [(XTm[:, k, m * 128:(m + 1) * 128], Wv[:, k, 0:128]) for k in range(16)])
                        B.copy("act", vb[:, m4 * 4:(m4 + 1) * 4, :], pb.rearrange("p (a b) -> p a b", a=4))
                    for m4 in range(4):
                        pb = B.bank()
                        nn_ = 4 if m4 < 3 else 3
                        for mm_ in range(nn_):
                            m = m4 * 4 + mm_
                            B.mm(pb[:, mm_ * 128:(mm_ + 1) * 128], [(XTm[:, k, 64 + m * 128:64 + (m + 1) * 128], Wv[:, k, 0:128]) for k in range(16)])
                        B.copy("act", vbs[:, m4 * 4:m4 * 4 + nn_, :], pb[:, 0:nn_ * 128].rearrange("p (a b) -> p a b", a=nn_))
                    pend = {}

                    def na_rows(rp):
                        out_ = []
                        for r in (2 * rp, 2 * rp + 1):
                            rs_ = min(max(r - 4, 0), 24)
                            out_.append((r, rs_, r - rs_))
                        return out_

                    def na_qk(rp, h_=h_, qbT=qbT, kbT=kbT, nabT=nabT, pend=pend):
                        rows = na_rows(rp)
                        ps = B.bank_lo()
                        for i, (r, rs_, v_) in enumerate(rows):
                            ks = rs_ * 64
                            for kt in range(4):
                                B.mm1(ps[:, i * 256 + kt * 64:i * 256 + (kt + 1) * 64],
                                      kbT[:, ks + kt * 128:ks + (kt + 1) * 128], qbT[:, r * 64:(r + 1) * 64], True, True)
                        tmp = B.sb(WORK + 24576 + (rp % 2) * 2048, [128, 2, 256], F32)
                        Pt = B.sb(WORK + 28672 + (rp % 2) * 1024, [128, 512], BF16)
                        v0, v1 = rows[0][2], rows[1][2]
                        if v0 == v1:
                            bias = nabT[:, v0, :].unsqueeze(1).broadcast_to([128, 2, 256])
                        else:
                            assert v1 == v0 + 1
                            bias = nabT[:, v0:v0 + 2, :]
                        B.stt(tmp, ps.rearrange("p (a b) -> p a b", a=2), SCALE, bias, ALU.mult, ALU.add)
                        B.act(Pt, tmp.rearrange("p a b -> p (a b)"), AF.Exp)
                        pend[rp] = Pt

                    na_qk(0)
                    for rp in range(16):
                        if rp + 1 < 16:
                            na_qk(rp + 1)
                        rows = na_rows(rp)
                        Pt = pend.pop(rp)
                        po = B.bank_hi()
                        for i, (r, rs_, v_) in enumerate(rows):
                            for kt in range(4):
                                Vt = vb[:, rs_ // 2 + kt, :] if rs_ % 2 == 0 else vbs[:, (rs_ - 1) // 2 + kt, :]
                                B.mm1(po[:, i * 128:i * 128 + 64], Vt, Pt[:, i * 256 + kt * 64:i * 256 + (kt + 1) * 64], kt == 0, kt == 3)
                            for kt in range(4):
                                B.mm1(po[:, i * 128 + 64:i * 128 + 128], onesb, Pt[:, i * 256 + kt * 64:i * 256 + (kt + 1) * 64], kt == 0, kt == 3)
                        rden = B.sb(WORK + 30720 + (rp % 2) * 512, [128, 2, 64], F32)
                        po3 = po[:, 0:256].rearrange("p (a b) -> p a b", a=2)
                        B.recip(rden, po3[:, :, 64:128])
                        r0 = 2 * rp
                        B.tt("dve", obT[:, h_, r0 * 64:(r0 + 2) * 64].rearrange("p (a b) -> p a b", a=2),
                             po3[:, :, 0:64], rden, ALU.mult)

                if stop == 'm_B':
                    return
                for hp in range(2):
                    qmT = B.sb(WORK, [128, 2, S], BF16)
                    Wq = B.sb(B.ring_slot(), [128, 16, 256], BF16)
                    B.dma("pool", Wq, winv[:, :, 3072 + hp * 256:3072 + (hp + 1) * 256])
                    for hh in range(2):
                        for t in range(4):
                            pb = B.bank()
                            B.mm(pb, [(Wq[:, k, hh * 128:(hh + 1) * 128], XTm[:, k, t * 512:(t + 1) * 512]) for k in range(16)])
                            B.copy("act" if t % 2 else "dve", qmT[:, hh, t * 512:(t + 1) * 512], pb)
                    blocks = []
                    for hh in range(2):
                        h_ = hp * 2 + hh
                        for qb in range(4):
                            blocks.append(dict(q=qmT[:, hh, qb * 512:(qb + 1) * 512],
                                               k=[kmT[:, h_, kt * 128:(kt + 1) * 128] for kt in range(2)],
                                               v=[vm[:, kt, h_ * 128:(h_ + 1) * 128] for kt in range(2)],
                                               dst=omT[:, h_, qb * 512:(qb + 1) * 512]))
                    attention_stream(blocks, WORK + 8192, WORK + 12288)

                if stop == 'm_M':
                    return
                def gload(c):
                    W1 = B.sb(B.ring_slot(), [128, 16, 256], BF16)
                    W2 = B.sb(B.ring_slot(), [128, 2, 16, 128], BF16)
                    B.dma("pool", W1[:, :, 0:128], winv[:, :, 3584 + c * 128:3584 + (c + 1) * 128])
                    B.dma("pool", W1[:, :, 128:256], winv[:, :, 3584 + 2048 + c * 128:3584 + 2048 + (c + 1) * 128])
                    B.dma("pool", W2[:, 0], winv[:, :, 3584 + 4096 + c * 128:3584 + 4096 + (c + 1) * 128])
                    B.dma("pool", W2[:, 1, 0:8], w_oa[:, c * 128:(c + 1) * 128].rearrange("(a p) n -> p a n", p=128))
                    B.dma("pool", W2[:, 1, 8:12], w_ob[:, c * 128:(c + 1) * 128].rearrange("(a p) n -> p a n", p=128))
                    B.dma("pool", W2[:, 1, 12:16], w_om[:, c * 128:(c + 1) * 128].rearrange("(a p) n -> p a n", p=128))
                    return W1, W2

                gw = {0: gload(0)}
                for c in range(16):
                    W1, W2 = gw.pop(c)
                    if c + 1 < 16:
                        gw[c + 1] = gload(c + 1)
                    yst = B.sb(WORK + 20480 + (c % 2) * 4096, [128, S], BF16)
                    for t in range(4):
                        tb = slice(t * 512, (t + 1) * 512)
                        e = t % 2
                        pg = [B.bank() for _ in range(3)]
                        B.mm(pg[0], [(W1[:, k, 0:128], XTm[:, k, tb]) for k in range(16)])
                        B.mm(pg[1], [(W1[:, k, 128:256], XTm[:, k, tb]) for k in range(16)])
                        B.mm(pg[2], [(W2[:, 0, k, :], XTm[:, k, tb]) for k in range(16)])
                        pp = [B.bank() for _ in range(3)]
                        B.mm(pp[0], [(W2[:, 1, a, :], oaT[:, a, tb]) for a in range(8)])
                        B.mm(pp[1], [(W2[:, 1, 8 + a, :], obT[:, a, tb]) for a in range(4)])
                        B.mm(pp[2], [(W2[:, 1, 12 + a, :], omT[:, a, tb]) for a in range(4)])
                        sg = [B.sb(WORK + e * 6144 + i * 2048, [128, 512], F32) for i in range(3)]
                        for i in range(3):
                            B.act(sg[i], pg[i], AF.Sigmoid, bias=bgate[:, i * 16 + c:i * 16 + c + 1])
                        ta = B.sb(WORK + 12288 + e * 2048, [128, 512], F32)
                        tb2 = B.sb(WORK + 16384 + e * 2048, [128, 512], F32)
                        B.tt("dve", ta, sg[0], pp[0], ALU.mult)
                        B.tt("dve", tb2, sg[1], pp[1], ALU.mult)
                        B.tt("pool", ta, ta, tb2, ALU.add)
                        B.tt("dve", tb2, sg[2], pp[2], ALU.mult)
                        B.tt("pool", yst[:, tb], ta, tb2, ALU.add)
                        B.dma("sp", yT_scr[c][:, tb], yst[:, tb], writes=[("yT", c, t)])

                for t in range(4):
                    B.dma("sp", XT[:, :, t * 512:(t + 1) * 512],
                          yT_scr[:, :, t * 512:(t + 1) * 512].rearrange("c p s -> p c s"),
                          reads=[("yT", c, t) for c in range(16)])
                wts = {}

                def wload(n):
                    WA = B.sb(B.ring_slot(), [128, 8, 512], BF16)
                    WB = B.sb(B.ring_slot(), [128, 8, 512], BF16)
                    B.dma("pool", WA, w_out[0:1024, n * 512:(n + 1) * 512].rearrange("(j p) n -> p j n", p=128))
                    B.dma("pool", WB, w_out[1024:2048, n * 512:(n + 1) * 512].rearrange("(j p) n -> p j n", p=128))
                    wts[n] = (WA, WB)

                def mmf(pb, n, m):
                    WA, WB = wts[n]
                    pairs = [(XT[:, j, m * 128:(m + 1) * 128], WA[:, j, :]) for j in range(8)]
                    pairs += [(XT[:, 8 + j, m * 128:(m + 1) * 128], WB[:, j, :]) for j in range(8)]
                    B.mm(pb, pairs)

                proj_phase(mmf, wload, x1_scr, lambda m, n: [("x1", m)], ALPHA, True, 0)
                g, b = load_gb(1)
                ln_pass(r_scr, x2_scr, g, b, "x2", True)

            load_x_to_XT(x, mem_prep)
            if not SKIP_FFN1:
                ffn(0, x, "xin", x1_scr, "x1", 0, True)
            final_src = None
            if STOP_AFTER == "ffn1":
                self.final_key = "x1"
                final_src = x1_scr
            elif STOP_AFTER == "ffn2x":
                load_x_to_XT(x1_scr) if False else None
                ffn(1, x1_scr, "x1", x2_scr, "x2", 2, False)
                self.final_key = "x2"
                final_src = x2_scr
            elif STOP_AFTER in ("m_prep", "m_Aproj", "m_A", "m_B", "m_M"):
                mixer(STOP_AFTER)
                oall = B.sb(ACTV, [128, 16, S], BF16)
                for a in range(16):
                    B.dma("pool", out[a * 128:(a + 1) * 128, :], oall[:, a, :], writes=[("out", a)])
            else:
                mixer()
                if STOP_AFTER == "mixer":
                    self.final_key = "x2"
                    final_src = x2_scr
                else:
                    ffn(1, x2_scr, "x2", out, "out", 2, False)

            if final_src is not None:
                for m in range(16):
                    R = B.sb(WORK + (m % 2) * 8192, [128, D], F32)
                    B.dma("sp", R, final_src[m * 128:(m + 1) * 128, :], reads=[(self.final_key, m)])
                    B.dma("sp", out[m * 128:(m + 1) * 128, :], R, writes=[("out", m)])
            B.P.add("sp", lambda h: h.nop(), reads=[("out", m) for m in range(16)], writes=[("done",)])

            esem = {"pe": s_pe, "act": s_act, "dve": s_dve, "pool": s_pool, "sp": s_sp}
            import contextlib
            with contextlib.ExitStack() as st:
                dsems = {}
                for q in ("pool", "sp", "act"):
                    for i in range(Prog.NDSEM):
                        if B.P.dma_count[q] > i:
                            dsems[(q, i)] = st.enter_context(nc.semaphore("d_%s_%d" % (q, i)))
                with nc.Block() as block:
                    handles = {}

                    @block.tensor
                    def _(h):
                        handles["pe"] = h
                        B.P.emit(handles, esem, dsems)("pe")

                    @block.scalar
                    def _(h):
                        handles["act"] = h
                        B.P.emit(handles, esem, dsems)("act")

                    @block.vector
                    def _(h):
                        handles["dve"] = h
                        B.P.emit(handles, esem, dsems)("dve")

                    @block.gpsimd
                    def _(h):
                        handles["pool"] = h
                        B.P.emit(handles, esem, dsems)("pool")

                    @block.sync
                    def _(h):
                        handles["sp"] = h
                        B.P.emit(handles, esem, dsems)("sp")
        return nc


def _rope_tables():
    t = np.arange(S)
    row = (t // 64).astype(np.float32)
    col = (t % 64).astype(np.float32)
    inv = (np.float32(10000.0) ** (-np.arange(0, 64, 2, dtype=np.float32) / np.float32(64))).astype(np.float32)
    ar = (row[:, None] * inv[None, :]).astype(np.float32)
    ac = (col[:, None] * inv[None, :]).astype(np.float32)
    cr, sr, cc, sc = np.cos(ar), np.sin(ar), np.cos(ac), np.sin(ac)
    C = np.concatenate([cr, cr, cc, cc], axis=1)
    Sg = np.concatenate([-sr, sr, -sc, sc], axis=1)
    return np.ascontiguousarray(np.concatenate([C, Sg], axis=1).astype(np.float32))


def _na_bias_layout(rpb):
    cols = np.arange(64)
    cs = np.clip(cols - 8, 0, 48)
    kr = np.arange(8)
    kc = np.arange(64)
    valid = (kc[:, None] >= cs[None, :]) & (kc[:, None] < cs[None, :] + 16)
    dc = np.clip(kc[:, None] - cols[None, :] + 15, 0, 30)
    outp = np.full((4, 8, 8, 64, 64), NEG, dtype=np.float32)
    for v in range(8):
        dr = kr - v + 7
        g = rpb[:, dr][:, :, dc]
        outp[:, v] = np.where(valid[None, None], g, np.float32(NEG))
    o = outp.reshape(4, 8, 4, 128, 64)
    o = o.transpose(0, 3, 1, 2, 4)
    return np.ascontiguousarray(o.reshape(4, 128, 8 * 4 * 64))


_CACHE = {}
INPUT_NAMES = ("x", "mem", "ln1_g", "ln1_b", "ffn1_w_gu", "ffn1_w_down", "w_in", "b_gate", "q_norm_a", "k_norm_a",
               "na_rpb", "w_mem_kv", "w_oa", "w_ob", "w_om", "w_out", "ln2_g", "ln2_b", "ffn2_w_gu", "ffn2_w_down",
               "ln3_g", "ln3_b")


def kernel(**inputs):
    f = lambda k: np.ascontiguousarray(np.asarray(inputs[k], dtype=np.float32))
    if "nc" not in _CACHE:
        _CACHE["nc"] = Builder().build()
    nc = _CACHE["nc"]
    x = f("x")
    mem = f("mem")
    shared = {}
    for i in (1, 2, 3):
        shared["ln%d_g" % i] = f("ln%d_g" % i).reshape(1, D)
        shared["ln%d_b" % i] = f("ln%d_b" % i).reshape(1, D)
    for i in (1, 2):
        shared["ffn%d_w_gu" % i] = f("ffn%d_w_gu" % i).reshape(D, 2 * DFF)
        shared["ffn%d_w_down" % i] = f("ffn%d_w_down" % i).reshape(DFF, D)
    shared["w_in"] = f("w_in").reshape(D, 9728)
    shared["b_gate"] = np.ascontiguousarray(f("b_gate").reshape(48, 128).T)
    qg = f("q_norm_a").reshape(128)
    kg = f("k_norm_a").reshape(128)
    shared["qk_gain"] = np.ascontiguousarray(np.concatenate([qg, qg, qg, qg, kg]).reshape(1, 640))
    shared["nab"] = _na_bias_layout(f("na_rpb").reshape(4, 15, 31))
    shared["w_mem_kv"] = f("w_mem_kv").reshape(D, 1024)
    shared["w_oa"] = f("w_oa").reshape(1024, D)
    shared["w_ob"] = f("w_ob").reshape(512, D)
    shared["w_om"] = f("w_om").reshape(512, D)
    shared["w_out"] = f("w_out").reshape(D, D)
    shared["ident"] = np.eye(128, dtype=np.float32)
    shared["rope"] = _rope_tables()
    in_maps = []
    for c in range(NCORES):
        m = dict(shared)
        m["x"] = np.ascontiguousarray(x[c])
        m["mem"] = np.ascontiguousarray(mem[c])
        in_maps.append(m)
    res = run_bass_kernel_spmd(nc, in_maps, core_ids=list(range(NCORES)))
    return np.stack([np.asarray(r["out"], dtype=np.float32) for r in res.results], axis=0)
```

```python
import numpy as np
import concourse.bass as bass
import concourse.mybir as mybir
from concourse.bass_utils import run_bass_kernel_spmd

F32 = mybir.dt.float32
BF16 = mybir.dt.bfloat16
AF = mybir.ActivationFunctionType
ALU = mybir.AluOpType

S = 2048
D = 2048
DFF = 5632
NCORES = 8
ALPHA = float(2.0 ** 0.25)
LN_EPS = 1e-5
RMS_EPS = 1e-6
SCALE = float(128 ** -0.5)
NEG = -30000.0
SKIP_FFN1 = False
STOP_AFTER = None

ITEM = {F32: 4, BF16: 2}


class Op:
    __slots__ = ("eng", "fn", "deps", "signal", "seq", "is_dma", "dsem", "dval", "idx")

    def __init__(self, eng, fn, is_dma):
        self.eng = eng
        self.fn = fn
        self.deps = []
        self.signal = False
        self.seq = 0
        self.is_dma = is_dma
        self.dsem = None
        self.dval = 0


GRAN = 512


def ap_keys(ap):
    tname = type(ap.tensor).__name__
    if tname.startswith("SB"):
        space = "s"
    elif tname.startswith("PSum"):
        space = "p"
    else:
        return []
    isz = ITEM[ap.dtype]
    pat = ap.ap
    pstep = pat[0][0]
    off = ap.offset % pstep if pstep > 0 else ap.offset
    if space == "p":
        return {("p", (off * isz) // 2048)}
    dims = [(s, c) for (s, c) in pat[1:] if c > 1]
    if not dims:
        dims = [(1, 1)]
    inner_s, inner_c = dims[-1]
    outer = dims[:-1]
    n_outer = 1
    for (_, c) in outer:
        n_outer *= c
    keys = set()
    if n_outer > 256:
        hi = off + sum((c - 1) * s for (s, c) in dims) + 1
        for g in range((off * isz) // GRAN, (hi * isz - 1) // GRAN + 1):
            keys.add((space, g))
        return keys
    offs = [off]
    for (s, c) in outer:
        offs = [o + i * s for o in offs for i in range(c)]
    ext = (inner_c - 1) * inner_s + 1
    for o in offs:
        for g in range((o * isz) // GRAN, ((o + ext) * isz - 1) // GRAN + 1):
            keys.add((space, g))
    return keys


class Prog:
    ENGS = ("pe", "act", "dve", "pool", "sp")
    NDSEM = 24

    def __init__(self, nc):
        self.nc = nc
        self.ops = {e: [] for e in self.ENGS}
        self.last_writer = {}
        self.readers = {}
        self.dma_count = {"pool": 0, "sp": 0, "act": 0}
        self.dma_last = {}
        self.n = 0

    def _keys(self, items):
        ks = set()
        for it in items:
            if isinstance(it, tuple):
                ks.add(it)
            else:
                ks.update(ap_keys(it))
        return ks

    def add(self, eng, fn, reads=(), writes=(), dma=False):
        op = Op(eng, fn, dma)
        op.idx = self.n
        self.n += 1
        rk = self._keys(reads)
        wk = self._keys(writes)
        deps = {}
        war = {}
        for k in rk:
            w = self.last_writer.get(k)
            if w is not None:
                deps[id(w)] = w
        for k in wk:
            w = self.last_writer.get(k)
            if w is not None:
                deps[id(w)] = w
            for r in self.readers.get(k, {}).values():
                war[id(r)] = r
        if dma:
            q = eng
            n = self.dma_count[q]
            self.dma_count[q] = n + 1
            slot = (q, n % self.NDSEM)
            prev = self.dma_last.get(slot)
            if prev is not None:
                deps[id(prev)] = prev
                op.dval = prev.dval + 16
            else:
                op.dval = 16
            op.dsem = slot
            self.dma_last[slot] = op
        for d in deps.values():
            if d is op:
                continue
            if d.is_dma:
                op.deps.append(d)
            elif d.eng != eng or dma or eng != "pe":
                d.signal = True
                op.deps.append(d)
        for d in war.values():
            if d is op or id(d) in deps:
                continue
            if d.is_dma:
                op.deps.append(d)
            elif d.eng != eng or dma:
                d.signal = True
                op.deps.append(d)
        for k in wk:
            self.last_writer[k] = op
            self.readers[k] = {}
        for k in rk:
            if k in wk:
                continue
            self.readers.setdefault(k, {})[eng + ("_d%d" % (op.dsem[1]) if dma else "")] = op
        self.ops[eng].append(op)
        return op

    def emit(self, handles, esem, dsems):
        for e in self.ENGS:
            c = 0
            for op in self.ops[e]:
                if op.signal and not op.is_dma:
                    c += 1
                op.seq = c

        def run(e):
            h = handles[e]
            waited = {}
            for op in self.ops[e]:
                need = {}
                for d in op.deps:
                    if d.is_dma:
                        key = ("d",) + d.dsem
                        sem = dsems[d.dsem]
                        val = d.dval
                    else:
                        key = ("e", d.eng)
                        sem = esem[d.eng]
                        val = d.seq
                    if need.get(key, (None, 0))[1] < val:
                        need[key] = (sem, val)
                for key, (sem, val) in need.items():
                    if waited.get(key, 0) < val:
                        h.wait_ge(sem, val)
                        waited[key] = val
                ins = op.fn(h)
                if op.is_dma:
                    ins.then_inc(dsems[op.dsem], 16)
                elif op.signal:
                    ins.then_inc(esem[e], 1)
        return run


class Builder:
    def __init__(self):
        self.nc = bass.Bass("TRN2", target_bir_lowering=False)
        self.P = Prog(self.nc)
        self.bank_ctr = 0
        self.lo_ctr = 0
        self.hi_ctr = 0
        self.ring_ctr = 0

    def din(self, name, shape, dt=F32):
        return self.nc.dram_tensor(name, list(shape), dt, kind="ExternalInput").ap()

    def dscr(self, name, shape, dt=F32):
        return self.nc.dram_tensor(name, list(shape), dt, kind="Internal").ap()

    def sb(self, off, shape, dt):
        n = 1
        for s in shape[1:]:
            n *= s
        nb = n * ITEM[dt]
        assert off % 4 == 0 and off + nb <= self.arena_bytes, (off, nb)
        v = self.arena[:, off // 2: (off + nb) // 2]
        if dt == F32:
            v = v.bitcast(F32)
        if len(shape) == 3:
            v = v.rearrange("p (a b) -> p a b", a=shape[1])
        elif len(shape) == 4:
            v = v.rearrange("p (a b c) -> p a b c", a=shape[1], b=shape[2])
        if shape[0] != 128:
            v = v[0:shape[0]]
        return v

    def bank(self, dt=F32):
        b = self.bank_ctr % 8
        self.bank_ctr += 1
        v = self.psum[:, b, :]
        if dt == BF16:
            v = v.bitcast(BF16)
        return v

    def bank_lo(self, dt=F32):
        b = self.lo_ctr % 4
        self.lo_ctr += 1
        v = self.psum[:, b, :]
        if dt == BF16:
            v = v.bitcast(BF16)
        return v

    def bank_hi(self):
        b = 4 + self.hi_ctr % 4
        self.hi_ctr += 1
        return self.psum[:, b, :]

    def mm1(self, out, lhsT, rhs, start, stop):
        return self.P.add("pe", lambda h: h.matmul(out, lhsT, rhs, start=start, stop=stop),
                          reads=[lhsT, rhs], writes=[out])

    def recip(self, out, in_):
        return self.P.add("dve", lambda h: h.reciprocal(out, in_), reads=[in_], writes=[out])

    def dma(self, q, out, in_, reads=(), writes=()):
        return self.P.add(q, lambda h, o=out, i=in_: h.dma_start(out=o, in_=i),
                          reads=list(reads) + [in_], writes=list(writes) + [out], dma=True)

    def mm(self, out, pairs, extra_reads=()):
        def fn(h, out=out, pairs=pairs):
            n = len(pairs)
            ins = None
            for i, (l, r) in enumerate(pairs):
                ins = h.matmul(out, l, r, start=(i == 0), stop=(i == n - 1))
            return ins
        reads = []
        for (l, r) in pairs:
            reads.append(l)
            reads.append(r)
        return self.P.add("pe", fn, reads=reads + list(extra_reads), writes=[out])

    def tr(self, out, in_, ident):
        return self.P.add("pe", lambda h: h.transpose(out, in_, ident), reads=[in_, ident], writes=[out])

    def act(self, out, in_, func, bias=None, scale=None, accum_out=None, extra_reads=()):
        kw = {}
        reads = [in_] + list(extra_reads)
        if bias is not None:
            kw["bias"] = bias
            if not isinstance(bias, (int, float)):
                reads.append(bias)
        if scale is not None:
            kw["scale"] = scale
            if not isinstance(scale, (int, float)):
                reads.append(scale)
        writes = [out]
        if accum_out is not None:
            kw["accum_out"] = accum_out
            writes.append(accum_out)
        return self.P.add("act", lambda h: h.activation(out, in_, func, **kw), reads=reads, writes=writes)

    def tt(self, eng, out, in0, in1, op):
        return self.P.add(eng, lambda h: h.tensor_tensor(out, in0, in1, op), reads=[in0, in1], writes=[out])

    def stt(self, out, in0, scalar, in1, op0, op1):
        reads = [in0, in1]
        if not isinstance(scalar, (int, float)):
            reads.append(scalar)
        return self.P.add("dve", lambda h: h.scalar_tensor_tensor(out, in0, scalar, in1, op0, op1),
                          reads=reads, writes=[out])

    def ts(self, eng, out, in0, s1, s2, op0, op1=None):
        reads = [in0]
        for s in (s1, s2):
            if s is not None and not isinstance(s, (int, float)):
                reads.append(s)
        if op1 is None:
            return self.P.add(eng, lambda h: h.tensor_scalar(out, in0, s1, None, op0), reads=reads, writes=[out])
        return self.P.add(eng, lambda h: h.tensor_scalar(out, in0, s1, s2, op0, op1), reads=reads, writes=[out])

    def copy(self, eng, out, in_):
        if eng == "act":
            return self.P.add("act", lambda h: h.copy(out, in_), reads=[in_], writes=[out])
        return self.P.add(eng, lambda h: h.tensor_copy(out, in_), reads=[in_], writes=[out])

    def ring_slot(self):
        s = self.ring_ctr % 4
        self.ring_ctr += 1
        return self.RING + s * 8192

    def build(self):
        nc = self.nc
        B = self
        x = B.din("x", [S, D])
        mem = B.din("mem", [256, D])
        ln_g = [B.din("ln%d_g" % i, [1, D]) for i in (1, 2, 3)]
        ln_b = [B.din("ln%d_b" % i, [1, D]) for i in (1, 2, 3)]
        w_gu = [B.din("ffn%d_w_gu" % i, [D, 2 * DFF]) for i in (1, 2)]
        w_dn = [B.din("ffn%d_w_down" % i, [DFF, D]) for i in (1, 2)]
        w_in = B.din("w_in", [D, 9728])
        b_gate = B.din("b_gate", [128, 48])
        qk_gain = B.din("qk_gain", [1, 640])
        nab = B.din("nab", [4, 128, 2048])
        w_mem_kv = B.din("w_mem_kv", [D, 1024])
        w_oa = B.din("w_oa", [1024, D])
        w_ob = B.din("w_ob", [512, D])
        w_om = B.din("w_om", [512, D])
        w_out = B.din("w_out", [D, D])
        ident_d = B.din("ident", [128, 128])
        rope_d = B.din("rope", [S, 256])
        out = nc.dram_tensor("out", [S, D], F32, kind="ExternalOutput").ap()
        r_scr = B.dscr("r_scr", [S, D])
        x1_scr = B.dscr("x1_scr", [S, D])
        x2_scr = B.dscr("x2_scr", [S, D])
        yT_scr = B.dscr("yT_scr", [16, 128, S], BF16)
        self.dbg = None

        self.arena_bytes = 204 * 1024
        XT_OFF = 0
        ACTV = 65536
        self.RING = 131072
        WORK = 163840
        CONST = 196608
        GB = ACTV + 49152

        with (
            nc.sbuf_tensor("arena", [128, self.arena_bytes // 2], BF16) as arena,
            nc.psum_tensor("psum", [128, 8, 512], F32) as psum,
            nc.semaphore("s_pe") as s_pe,
            nc.semaphore("s_act") as s_act,
            nc.semaphore("s_dve") as s_dve,
            nc.semaphore("s_pool") as s_pool,
            nc.semaphore("s_sp") as s_sp,
        ):
            self.arena = arena
            self.psum = psum
            XT = B.sb(XT_OFF, [128, 16, S], BF16)
            ident = B.sb(CONST, [128, 128], F32)
            identb = B.sb(CONST + 512, [128, 128], BF16)
            onesb = B.sb(CONST + 768, [128, 128], BF16)
            stats = B.sb(CONST + 1024, [128, 16, 4, 6], F32)
            mv = B.sb(CONST + 2560, [128, 16, 2], F32)
            smal = B.sb(CONST + 2688, [128, 16, 4], F32)
            bgate = B.sb(CONST + 2944, [128, 48], F32)
            gains = B.sb(CONST + 3136, [128, 640], F32)
            epsT = B.sb(CONST + 9792, [128, 1], F32)
            KM_OFF = CONST + 5696
            VM_OFF = KM_OFF + 2048
            assert VM_OFF + 2048 <= self.arena_bytes

            B.dma("sp", ident, ident_d)
            B.copy("dve", identb, ident)
            B.P.add("pool", lambda h: h.memset(onesb, 1.0), writes=[onesb])
            B.P.add("pool", lambda h: h.memset(epsT, LN_EPS), writes=[epsT])

            def transpose_tile(R, m, evac=("act", "dve")):
                for q in range(4):
                    pb = B.bank()
                    for kk in range(4):
                        k = q * 4 + kk
                        B.tr(pb[:, kk * 128:(kk + 1) * 128], R[:, k * 128:(k + 1) * 128], ident)
                    dst = XT[:, q * 4:(q + 1) * 4, m * 128:(m + 1) * 128]
                    src = pb.rearrange("p (a b) -> p a b", a=4)
                    B.copy(evac[q % 2], dst, src)

            def load_x_to_XT(src_dram, mid_fn=None):
                for m in range(16):
                    if m == 8 and mid_fn is not None:
                        mid_fn()
                    R = B.sb(ACTV + (m % 4) * 8192, [128, D], F32)
                    B.dma("sp", R, src_dram[m * 128:(m + 1) * 128, :])
                    transpose_tile(R, m)

            def load_gb(i):
                g = B.sb(GB, [128, D], F32)
                b = B.sb(GB + 8192, [128, D], F32)
                B.dma("sp", g, ln_g[i].broadcast_to([128, D]))
                B.dma("sp", b, ln_b[i].broadcast_to([128, D]))
                return g, b

            def ln_pass(src_scr, dst_dram, g, b, dst_key, do_transpose):
                def Rt(m):
                    return B.sb(ACTV + (m % 6) * 8192, [128, D], F32)

                def ld(m):
                    B.dma("sp", Rt(m), src_scr[m * 128:(m + 1) * 128, :], reads=[("r", m, n) for n in range(4)])

                def st_stage(m):
                    mvm = mv[:, m, :]
                    B.P.add("dve", lambda h, o=mvm, i=stats[:, m].rearrange("p a b -> p (a b)"): h.bn_aggr(o, i),
                            reads=[stats[:, m]], writes=[mvm])
                    sd = smal[:, m, 0:1]
                    B.ts("dve", sd, mv[:, m, 1:2], LN_EPS, None, ALU.add)
                    B.act(sd, sd, AF.Sqrt)

                def nrm_stage(m):
                    R = Rt(m)
                    sd = smal[:, m, 0:1]
                    rstd = smal[:, m, 1:2]
                    B.recip(rstd, sd)
                    B.stt(R, R, mv[:, m, 0:1], g, ALU.subtract, ALU.mult)
                    B.stt(R, R, rstd, b, ALU.mult, ALU.add)
                    B.dma("sp", dst_dram[m * 128:(m + 1) * 128, :], R, writes=[(dst_key, m)])
                    if do_transpose:
                        transpose_tile(R, m, evac=("act", "act"))

                for m in range(5):
                    ld(m)
                st_stage(0)
                st_stage(1)
                for m in range(16):
                    if m + 5 < 16:
                        ld(m + 5)
                    if m + 2 < 16:
                        st_stage(m + 2)
                    nrm_stage(m)

            EP_LOOK = 10
            EP_RING = 14

            def ep_load(res_dram, res_keys, m, n, piece_idx):
                rp = B.sb(WORK + (piece_idx % EP_RING) * 2048, [128, 512], F32)
                B.dma("sp", rp, res_dram[m * 128:(m + 1) * 128, n * 512:(n + 1) * 512], reads=res_keys)

            def ep_finish(pb, scale, m, n, last, out_scr, piece_idx):
                rp = B.sb(WORK + (piece_idx % EP_RING) * 2048, [128, 512], F32)
                if scale is None:
                    B.tt("dve", rp, rp, pb, ALU.add)
                else:
                    B.stt(rp, rp, scale, pb, ALU.mult, ALU.add)
                if last:
                    B.P.add("dve", lambda h, o=stats[:, m, n, :], i=rp: h.bn_stats(o, i),
                            reads=[rp], writes=[stats[:, m, n, :]])
                B.dma("sp", out_scr[m * 128:(m + 1) * 128, n * 512:(n + 1) * 512], rp, writes=[("r", m, n)])

            def proj_phase(mm_fn, wload_fn, res_dram, res_key_fn, scale, last, piece0):
                order = [(n, m) for n in range(4) for m in range(16)]
                for i in range(min(EP_LOOK, len(order))):
                    n, m = order[i]
                    ep_load(res_dram, res_key_fn(m, n), m, n, piece0 + i)
                for i, (n, m) in enumerate(order):
                    if m == 0:
                        wload_fn(n)
                    if i + EP_LOOK < len(order):
                        n2, m2 = order[i + EP_LOOK]
                        ep_load(res_dram, res_key_fn(m2, n2), m2, n2, piece0 + i + EP_LOOK)
                    pb = B.bank()
                    mm_fn(pb, n, m)
                    ep_finish(pb, scale, m, n, last, r_scr, piece0 + i)
                return piece0 + len(order)

            def ffn(fi, res_dram, res_key, dst_dram, dst_key, ln_i, do_transpose):
                wgu = w_gu[fi].rearrange("(k p) n -> p k n", p=128)
                wdn = w_dn[fi]
                g, b = load_gb(ln_i)
                parts = [6, 6, 6, 4]
                tile0 = 0
                piece = 0
                for pi, ntile in enumerate(parts):
                    nch = ntile * 2
                    HT = B.sb(ACTV, [128, 12, S], BF16)
                    for ti in range(ntile):
                        c0 = (tile0 + ti) * 256
                        Wg = B.sb(B.ring_slot(), [128, 16, 256], BF16)
                        Wu = B.sb(B.ring_slot(), [128, 16, 256], BF16)
                        B.dma("pool", Wg, wgu[:, :, c0:c0 + 256])
                        B.dma("pool", Wu, wgu[:, :, DFF + c0:DFF + c0 + 256])
                        for jj in range(2):
                            j = ti * 2 + jj
                            for t in range(4):
                                pg = B.bank()
                                pu = B.bank()
                                B.mm(pg, [(Wg[:, k, jj * 128:(jj + 1) * 128], XT[:, k, t * 512:(t + 1) * 512]) for k in range(16)])
                                B.mm(pu, [(Wu[:, k, jj * 128:(jj + 1) * 128], XT[:, k, t * 512:(t + 1) * 512]) for k in range(16)])
                                sg = B.sb(WORK + 28672 + ((j * 4 + t) % 2) * 2048, [128, 512], F32)
                                B.act(sg, pg, AF.Silu)
                                B.stt(HT[:, j, t * 512:(t + 1) * 512], sg, 0.5, pu, ALU.mult, ALU.mult)
                    row0 = tile0 * 256
                    last = (pi == len(parts) - 1)
                    wts = {}

                    def wload(n, row0=row0, nch=nch, wts=wts):
                        WA = B.sb(B.ring_slot(), [128, 8, 512], BF16)
                        WB = B.sb(B.ring_slot(), [128, 8, 512], BF16)
                        B.dma("pool", WA, wdn[row0:row0 + 1024, n * 512:(n + 1) * 512].rearrange("(j p) n -> p j n", p=128))
                        B.dma("pool", WB[:, 0:nch - 8, :], wdn[row0 + 1024:row0 + nch * 128, n * 512:(n + 1) * 512].rearrange("(j p) n -> p j n", p=128))
                        wts[n] = (WA, WB)

                    def mmf(pb, n, m, nch=nch, wts=wts, HT=HT):
                        WA, WB = wts[n]
                        pairs = [(HT[:, j, m * 128:(m + 1) * 128], WA[:, j, :]) for j in range(8)]
                        pairs += [(HT[:, 8 + j, m * 128:(m + 1) * 128], WB[:, j, :]) for j in range(nch - 8)]
                        B.mm(pb, pairs)

                    if last:
                        fused_last_part(HT, wdn, row0, dst_dram, dst_key, g, b, do_transpose)
                    elif pi == 0:
                        piece = proj_phase(mmf, wload, res_dram, lambda m, n: [(res_key, m)], ALPHA, False, piece)
                    else:
                        piece = proj_phase(mmf, wload, r_scr, lambda m, n: [("r", m, n)], None, False, piece)
                    tile0 += ntile

            def fused_last_part(HT, wdn, row0, dst_dram, dst_key, g, b, do_transpose):
                Ws = []
                for n in range(4):
                    W = B.sb(B.ring_slot(), [128, 8, 512], BF16)
                    B.dma("pool", W, wdn[row0:row0 + 1024, n * 512:(n + 1) * 512].rearrange("(j p) n -> p j n", p=128))
                    Ws.append(W)
                RO = [ACTV + 32768, ACTV + 40960, WORK, WORK + 8192, WORK + 20480]

                def Rt(m):
                    return B.sb(RO[m % 5], [128, D], F32)

                def ld(m):
                    B.dma("sp", Rt(m), r_scr[m * 128:(m + 1) * 128, :], reads=[("r", m, n) for n in range(4)])

                def stage_a(m):
                    R = Rt(m)
                    sm = stats[:, m].rearrange("p a b -> p (a b)")
                    pbs = []
                    for n in range(4):
                        pb = B.bank()
                        B.mm(pb, [(HT[:, j, m * 128:(m + 1) * 128], Ws[n][:, j, :]) for j in range(8)])
                        pbs.append(pb)
                    for n in range(4):
                        Rn = R[:, n * 512:(n + 1) * 512]
                        B.P.add("dve", lambda h, o=Rn, p_=pbs[n], a=sm[:, n:n + 1]:
                                h.scalar_tensor_tensor(o, p_, 1.0, o, ALU.mult, ALU.add, accum_out=a),
                                reads=[pbs[n], Rn], writes=[Rn, sm[:, n:n + 1]])
                    junk = B.sb(WORK + 16384, [128, D], BF16)
                    B.act(junk, R, AF.Square, accum_out=sm[:, 9:10])

                def stage_b(m):
                    R = Rt(m)
                    sm = stats[:, m].rearrange("p a b -> p (a b)")
                    t = sm[:, 8:9]
                    mean = sm[:, 10:11]
                    ex2 = sm[:, 11:12]
                    nvar = sm[:, 12:13]
                    ve = sm[:, 13:14]
                    rstd = sm[:, 14:15]
                    B.P.add("dve", lambda h: h.reduce_sum(t, sm[:, 0:4], mybir.AxisListType.X), reads=[sm[:, 0:4]], writes=[t])
                    B.ts("dve", sm[:, 10:12], sm[:, 8:10], 1.0 / D, None, ALU.mult)
                    B.stt(nvar, mean, mean, ex2, ALU.mult, ALU.subtract)
                    B.act(ve, nvar, AF.Sqrt, bias=epsT, scale=-1.0)
                    B.stt(R, R, mean, g, ALU.subtract, ALU.mult)
                    B.recip(rstd, ve)
                    B.act(R, R, AF.Copy, scale=rstd)
                    B.tt("pool", R, R, b, ALU.add)
                    B.dma("sp", dst_dram[m * 128:(m + 1) * 128, :], R, writes=[(dst_key, m)])

                ld(0)
                ld(1)
                for m in range(20):
                    if m >= 3 and m - 3 < 16 and do_transpose:
                        transpose_tile(Rt(m - 3), m - 3, evac=("act", "act"))
                    if m + 2 < 16:
                        ld(m + 2)
                    if m < 16:
                        stage_a(m)
                    if 1 <= m <= 16:
                        stage_b(m - 1)

            def mem_prep():
                kmT = B.sb(KM_OFF, [128, 4, 256], BF16)
                vm = B.sb(VM_OFF, [128, 2, 512], BF16)
                memT = B.sb(ACTV + 49152, [128, 16, 256], BF16)
                for i in range(2):
                    R = B.sb(ACTV + 32768 + i * 8192, [128, D], F32)
                    B.dma("sp", R, mem[i * 128:(i + 1) * 128, :])
                    for q in range(4):
                        pb = B.bank()
                        for kk in range(4):
                            k = q * 4 + kk
                            B.tr(pb[:, kk * 128:(kk + 1) * 128], R[:, k * 128:(k + 1) * 128], ident)
                        B.copy("act" if q % 2 == 0 else "dve", memT[:, q * 4:(q + 1) * 4, i * 128:(i + 1) * 128],
                               pb.rearrange("p (a b) -> p a b", a=4))
                wmv = w_mem_kv.rearrange("(k p) n -> p k n", p=128)
                for ti in range(4):
                    W = B.sb(B.ring_slot(), [128, 16, 256], BF16)
                    B.dma("pool", W, wmv[:, :, ti * 256:(ti + 1) * 256])
                    if ti < 2:
                        for hh in range(2):
                            h_ = ti * 2 + hh
                            pb = B.bank()
                            B.mm(pb[:, 0:256], [(W[:, k, hh * 128:(hh + 1) * 128], memT[:, k, :]) for k in range(16)])
                            B.copy("act", kmT[:, h_, :], pb[:, 0:256])
                    else:
                        c0 = (ti - 2) * 256
                        for kt in range(2):
                            pb = B.bank()
                            B.mm(pb[:, 0:256], [(memT[:, k, kt * 128:(kt + 1) * 128], W[:, k, :]) for k in range(16)])
                            B.copy("dve", vm[:, kt, c0:c0 + 256], pb[:, 0:256])


            def attention_stream(blocks, pt_base, rden_off, look=2):
                steps = []
                for bi, blk in enumerate(blocks):
                    for kt in range(len(blk["k"])):
                        steps.append((bi, kt))
                state = {}
                pts = {}
                cnt = [0]

                def qk(si):
                    bi, kt = steps[si]
                    blk = blocks[bi]
                    ps = B.bank_lo()
                    B.mm1(ps, blk["k"][kt], blk["q"], True, True)
                    Pt = B.sb(pt_base + (cnt[0] % 4) * 1024, [128, 512], BF16)
                    cnt[0] += 1
                    B.act(Pt, ps, AF.Exp, scale=SCALE)
                    pts[si] = Pt

                for si in range(min(look, len(steps))):
                    qk(si)
                for si, (bi, kt) in enumerate(steps):
                    blk = blocks[bi]
                    nk = len(blk["k"])
                    if si + look < len(steps):
                        qk(si + look)
                    if kt == 0:
                        state[bi] = (B.bank_hi(), B.bank_hi())
                    po, pd = state[bi]
                    Pt = pts.pop(si)
                    B.mm1(po, blk["v"][kt], Pt, kt == 0, kt == nk - 1)
                    B.mm1(pd, onesb, Pt, kt == 0, kt == nk - 1)
                    if kt == nk - 1:
                        rden = B.sb(rden_off + (bi % 2) * 2048, [128, 512], F32)
                        B.recip(rden, pd)
                        B.tt("dve", blk["dst"], po, rden, ALU.mult)
                        del state[bi]

            def mixer(stop=None):
                winv = w_in.rearrange("(k p) n -> p k n", p=128)
                oaT = B.sb(ACTV, [128, 8, S], BF16)
                obT = B.sb(ACTV + 32768, [128, 4, S], BF16)
                omT = B.sb(ACTV + 49152, [128, 4, S], BF16)
                kmT = B.sb(KM_OFF, [128, 4, 256], BF16)
                vm = B.sb(VM_OFF, [128, 2, 512], BF16)
                B.dma("sp", bgate, b_gate)
                B.dma("sp", gains, qk_gain.broadcast_to([128, 640]))
                XTm = XT

                if stop == 'm_prep':
                    return
                TA = ACTV + 32768
                for g_ in range(2):
                    qaT = B.sb(WORK, [128, 4, S], BF16)
                    kaT = B.sb(WORK + 16384, [128, S], BF16)
                    va = B.sb(WORK + 20480, [128, 16, 128], BF16)
                    Wq0 = B.sb(B.ring_slot(), [128, 16, 256], BF16)
                    Wq1 = B.sb(B.ring_slot(), [128, 16, 256], BF16)
                    Wkv = B.sb(B.ring_slot(), [128, 16, 256], BF16)
                    B.dma("pool", Wq0, winv[:, :, g_ * 512:g_ * 512 + 256])
                    B.dma("pool", Wq1, winv[:, :, g_ * 512 + 256:g_ * 512 + 512])
                    B.dma("pool", Wkv[:, :, 0:128], winv[:, :, 1024 + g_ * 128:1024 + (g_ + 1) * 128])
                    B.dma("pool", Wkv[:, :, 128:256], winv[:, :, 1280 + g_ * 128:1280 + (g_ + 1) * 128])
                    def a_stage1(m):
                        xs = lambda k: XTm[:, k, m * 128:(m + 1) * 128]
                        pq = B.bank()
                        B.mm(pq[:, 0:256], [(xs(k), Wq0[:, k, :]) for k in range(16)])
                        B.mm(pq[:, 256:512], [(xs(k), Wq1[:, k, :]) for k in range(16)])
                        pk = B.bank()
                        B.mm(pk[:, 0:256], [(xs(k), Wkv[:, k, :]) for k in range(16)])
                        base = TA + (m % 3) * 10240
                        zt = B.sb(base, [128, 640], F32)
                        t1 = B.sb(base + 2560, [128, 640], F32)
                        t2 = B.sb(base + 5120, [128, 640], F32)
                        qn = B.sb(base + 7680, [128, 640], BF16)
                        ropeT = B.sb(base + 8960, [128, 256], F32)
                        sm = B.sb(base + 9984, [128, 16], F32)
                        B.dma("sp", ropeT, rope_d[m * 128:(m + 1) * 128, :])
                        ssum = sm[:, 0:5]
                        rs = sm[:, 8:13]
                        for hh in range(4):
                            B.act(t1[:, hh * 128:(hh + 1) * 128], pq[:, hh * 128:(hh + 1) * 128], AF.Square, accum_out=ssum[:, hh:hh + 1])
                        B.act(t1[:, 512:640], pk[:, 0:128], AF.Square, accum_out=ssum[:, 4:5])
                        B.copy("act", zt[:, 0:512], pq)
                        B.copy("act", zt[:, 512:640], pk[:, 0:128])
                        B.copy("act", va[:, m, :], pk[:, 128:256])
                        B.ts("dve", ssum, ssum, 1.0 / 128.0, RMS_EPS, ALU.mult, ALU.add)
                        B.act(ssum, ssum, AF.Sqrt)
                        B.recip(rs, ssum)
                        z3 = zt.rearrange("p (a b) -> p a b", a=5)
                        B.tt("dve", zt, zt, gains, ALU.mult)
                        C3 = ropeT[:, 0:128].unsqueeze(1).broadcast_to([128, 5, 128])
                        B.tt("dve", t1.rearrange("p (a b) -> p a b", a=5), z3, C3, ALU.mult)
                        t23 = t2.rearrange("p (a b) -> p a b", a=5)
                        for blk in range(2):
                            for half in range(2):
                                o_ = blk * 64 + half * 32
                                i_ = blk * 64 + (1 - half) * 32
                                Sg = ropeT[:, 128 + o_:128 + o_ + 32].unsqueeze(1).broadcast_to([128, 5, 32])
                                B.tt("pool" if half else "dve", t23[:, :, o_:o_ + 32], z3[:, :, i_:i_ + 32], Sg, ALU.mult)
                        B.tt("pool", t1, t1, t2, ALU.add)
                        B.tt("dve", qn.rearrange("p (a b) -> p a b", a=5), t1.rearrange("p (a b) -> p a b", a=5),
                             rs.unsqueeze(2).broadcast_to([128, 5, 128]), ALU.mult)

                    def a_stage2(m):
                        qn = B.sb(TA + (m % 3) * 10240 + 7680, [128, 640], BF16)
                        pt = B.bank(BF16)
                        for hh in range(5):
                            B.tr(pt[:, hh * 128:(hh + 1) * 128], qn[:, hh * 128:(hh + 1) * 128], identb)
                        B.copy("act", qaT[:, :, m * 128:(m + 1) * 128], pt[:, 0:512].rearrange("p (a b) -> p a b", a=4))
                        B.copy("act", kaT[:, m * 128:(m + 1) * 128], pt[:, 512:640])

                    for m in range(18):
                        if m < 16:
                            a_stage1(m)
                        if m >= 2:
                            a_stage2(m - 2)
                    if stop == 'm_Aproj':
                        return
                    blocks = []
                    for hh in range(4):
                        for qb in range(4):
                            blocks.append(dict(q=qaT[:, hh, qb * 512:(qb + 1) * 512],
                                               k=[kaT[:, kt * 128:(kt + 1) * 128] for kt in range(16)],
                                               v=[va[:, kt, :] for kt in range(16)],
                                               dst=oaT[:, g_ * 4 + hh, qb * 512:(qb + 1) * 512]))
                    attention_stream(blocks, WORK + 24576, WORK + 28672)

                if stop == 'm_A':
                    return
                for h_ in range(4):
                    qbT = B.sb(WORK, [128, S], BF16)
                    kbT = B.sb(WORK + 4096, [128, S], BF16)
                    vb = B.sb(WORK + 8192, [128, 16, 128], BF16)
                    vbs = B.sb(WORK + 12288, [128, 15, 128], BF16)
                    nabT = B.sb(WORK + 16384, [128, 8, 256], F32)
                    B.dma("sp", nabT, nab[h_].rearrange("p (a b) -> p a b", a=8))
                    Wqk = B.sb(B.ring_slot(), [128, 16, 256], BF16)
                    Wv = B.sb(B.ring_slot(), [128, 16, 256], BF16)
                    B.dma("pool", Wqk[:, :, 0:128], winv[:, :, 1536 + h_ * 128:1536 + (h_ + 1) * 128])
                    B.dma("pool", Wqk[:, :, 128:256], winv[:, :, 2048 + h_ * 128:2048 + (h_ + 1) * 128])
                    B.dma("pool", Wv[:, :, 0:128], winv[:, :, 2560 + h_ * 128:2560 + (h_ + 1) * 128])
                    for t in range(4):
                        pb = B.bank()
                        B.mm(pb, [(Wqk[:, k, 0:128], XTm[:, k, t * 512:(t + 1) * 512]) for k in range(16)])
                        B.copy("act", qbT[:, t * 512:(t + 1) * 512], pb)
                        pb = B.bank()
                        B.mm(pb, [(Wqk[:, k, 128:256], XTm[:, k, t * 512:(t + 1) * 512]) for k in range(16)])
                        B.copy("act", kbT[:, t * 512:(t + 1) * 512], pb)
                    for m4 in range(4):
                        pb = B.bank()
                        for mm_ in range(4):
                            m = m4 * 4 + mm_
                            B.mm(pb[:, mm_ * 128:(mm_ + 1) * 128], [(XTm[:, k, m * 128:(m + 1) * 128], Wv[:, k, 0:128]) for k in range(16)])
                        B.copy("act", vb[:, m4 * 4:(m4 + 1) * 4, :], pb.rearrange("p (a b) -> p a b", a=4))
                    for m4 in range(4):
                        pb = B.bank()
                        nn_ = 4 if m4 < 3 else 3
                        for mm_ in range(nn_):
                            m = m4 * 4 + mm_
                            B.mm(pb[:, mm_ * 128:(mm_ + 1) * 128], [(XTm[:, k, 64 + m * 128:64 + (m + 1) * 128], Wv[:, k, 0:128]) for k in range(16)])
                        B.copy("act", vbs[:, m4 * 4:m4 * 4 + nn_, :], pb[:, 0:nn_ * 128].rearrange("p (a b) -> p a b", a=nn_))
                    pend = {}

                    def na_rows(rp):
                        out_ = []
                        for r in (2 * rp, 2 * rp + 1):
                            rs_ = min(max(r - 4, 0), 24)
                            out_.append((r, rs_, r - rs_))
                        return out_

                    def na_qk(rp, h_=h_, qbT=qbT, kbT=kbT, nabT=nabT, pend=pend):
                        rows = na_rows(rp)
                        ps = B.bank_lo()
                        for i, (r, rs_, v_) in enumerate(rows):
                            ks = rs_ * 64
                            for kt in range(4):
                                B.mm1(ps[:, i * 256 + kt * 64:i * 256 + (kt + 1) * 64],
                                      kbT[:, ks + kt * 128:ks + (kt + 1) * 128], qbT[:, r * 64:(r + 1) * 64], True, True)
                        tmp = B.sb(WORK + 24576 + (rp % 2) * 2048, [128, 2, 256], F32)
                        Pt = B.sb(WORK + 28672 + (rp % 2) * 1024, [128, 512], BF16)
                        v0, v1 = rows[0][2], rows[1][2]
                        if v0 == v1:
                            bias = nabT[:, v0, :].unsqueeze(1).broadcast_to([128, 2, 256])
                        else:
                            assert v1 == v0 + 1
                            bias = nabT[:, v0:v0 + 2, :]
                        B.stt(tmp, ps.rearrange("p (a b) -> p a b", a=2), SCALE, bias, ALU.mult, ALU.add)
                        B.act(Pt, tmp.rearrange("p a b -> p (a b)"), AF.Exp)
                        pend[rp] = Pt

                    na_qk(0)
                    for rp in range(16):
                        if rp + 1 < 16:
                            na_qk(rp + 1)
                        rows = na_rows(rp)
                        Pt = pend.pop(rp)
                        po = B.bank_hi()
                        for i, (r, rs_, v_) in enumerate(rows):
                            for kt in range(4):
                                Vt = vb[:, rs_ // 2 + kt, :] if rs_ % 2 == 0 else vbs[:, (rs_ - 1) // 2 + kt, :]
                                B.mm1(po[:, i * 128:i * 128 + 64], Vt, Pt[:, i * 256 + kt * 64:i * 256 + (kt + 1) * 64], kt == 0, kt == 3)
                            for kt in range(4):
                                B.mm1(po[:, i * 128 + 64:i * 128 + 128], onesb, Pt[:, i * 256 + kt * 64:i * 256 + (kt + 1) * 64], kt == 0, kt == 3)
                        rden = B.sb(WORK + 30720 + (rp % 2) * 512, [128, 2, 64], F32)
                        po3 = po[:, 0:256].rearrange("p (a b) -> p a b", a=2)
                        B.recip(rden, po3[:, :, 64:128])
                        r0 = 2 * rp
                        B.tt("dve", obT[:, h_, r0 * 64:(r0 + 2) * 64].rearrange("p (a b) -> p a b", a=2),
                             po3[:, :, 0:64], rden, ALU.mult)

                if stop == 'm_B':
                    return
                for hp in range(2):
                    qmT = B.sb(WORK, [128, 2, S], BF16)
                    Wq = B.sb(B.ring_slot(), [128, 16, 256], BF16)
                    B.dma("pool", Wq, winv[:, :, 3072 + hp * 256:3072 + (hp + 1) * 256])
                    for hh in range(2):
                        for t in range(4):
                            pb = B.bank()
                            B.mm(pb, [(Wq[:, k, hh * 128:(hh + 1) * 128], XTm[:, k, t * 512:(t + 1) * 512]) for k in range(16)])
                            B.copy("act", qmT[:, hh, t * 512:(t + 1) * 512], pb)
                    blocks = []
                    for hh in range(2):
                        h_ = hp * 2 + hh
                        for qb in range(4):
                            blocks.append(dict(q=qmT[:, hh, qb * 512:(qb + 1) * 512],
                                               k=[kmT[:, h_, kt * 128:(kt + 1) * 128] for kt in range(2)],
                                               v=[vm[:, kt, h_ * 128:(h_ + 1) * 128] for kt in range(2)],
                                               dst=omT[:, h_, qb * 512:(qb + 1) * 512]))
                    attention_stream(blocks, WORK + 8192, WORK + 12288)

                if stop == 'm_M':
                    return
                def gload(c):
                    W1 = B.sb(B.ring_slot(), [128, 16, 256], BF16)
                    W2 = B.sb(B.ring_slot(), [128, 2, 16, 128], BF16)
                    B.dma("pool", W1[:, :, 0:128], winv[:, :, 3584 + c * 128:3584 + (c + 1) * 128])
                    B.dma("pool", W1[:, :, 128:256], winv[:, :, 3584 + 2048 + c * 128:3584 + 2048 + (c + 1) * 128])
                    B.dma("pool", W2[:, 0], winv[:, :, 3584 + 4096 + c * 128:3584 + 4096 + (c + 1) * 128])
                    B.dma("pool", W2[:, 1, 0:8], w_oa[:, c * 128:(c + 1) * 128].rearrange("(a p) n -> p a n", p=128))
                    B.dma("pool", W2[:, 1, 8:12], w_ob[:, c * 128:(c + 1) * 128].rearrange("(a p) n -> p a n", p=128))
                    B.dma("pool", W2[:, 1, 12:16], w_om[:, c * 128:(c + 1) * 128].rearrange("(a p) n -> p a n", p=128))
                    return W1, W2

                gw = {0: gload(0)}
                for c in range(16):
                    W1, W2 = gw.pop(c)
                    if c + 1 < 16:
                        gw[c + 1] = gload(c + 1)
                    yst = B.sb(WORK + 20480 + (c % 2) * 4096, [128, S], BF16)
                    for t in range(4):
                        tb = slice(t * 512, (t + 1) * 512)
                        e = t % 2
                        pg = [B.bank() for _ in range(3)]
                        B.mm(pg[0], [(W1[:, k, 0:128], XTm[:, k, tb]) for k in range(16)])
                        B.mm(pg[1], [(W1[:, k, 128:256], XTm[:, k, tb]) for k in range(16)])
                        B.mm(pg[2], [(W2[:, 0, k, :], XTm[:, k, tb]) for k in range(16)])
                        pp = [B.bank() for _ in range(3)]
                        B.mm(pp[0], [(W2[:, 1, a, :], oaT[:, a, tb]) for a in range(8)])
                        B.mm(pp[1], [(W2[:, 1, 8 + a, :], obT[:, a, tb]) for a in range(4)])
                        B.mm(pp[2], [(W2[:, 1, 12 + a, :], omT[:, a, tb]) for a in range(4)])
                        sg = [B.sb(WORK + e * 6144 + i * 2048, [128, 512], F32) for i in range(3)]
                        for i in range(3):
                            B.act(sg[i], pg[i], AF.Sigmoid, bias=bgate[:, i * 16 + c:i * 16 + c + 1])
                        ta = B.sb(WORK + 12288 + e * 2048, [128, 512], F32)
                        tb2 = B.sb(WORK + 16384 + e * 2048, [128, 512], F32)
                        B.tt("dve", ta, sg[0], pp[0], ALU.mult)
                        B.tt("dve", tb2, sg[1], pp[1], ALU.mult)
                        B.tt("pool", ta, ta, tb2, ALU.add)
                        B.tt("dve", tb2, sg[2], pp[2], ALU.mult)
                        B.tt("pool", yst[:, tb], ta, tb2, ALU.add)
                        B.dma("sp", yT_scr[c][:, tb], yst[:, tb], writes=[("yT", c, t)])

                for t in range(4):
                    B.dma("sp", XT[:, :, t * 512:(t + 1) * 512],
                          yT_scr[:, :, t * 512:(t + 1) * 512].rearrange("c p s -> p c s"),
                          reads=[("yT", c, t) for c in range(16)])
                wts = {}

                def wload(n):
                    WA = B.sb(B.ring_slot(), [128, 8, 512], BF16)
                    WB = B.sb(B.ring_slot(), [128, 8, 512], BF16)
                    B.dma("pool", WA, w_out[0:1024, n * 512:(n + 1) * 512].rearrange("(j p) n -> p j n", p=128))
                    B.dma("pool", WB, w_out[1024:2048, n * 512:(n + 1) * 512].rearrange("(j p) n -> p j n", p=128))
                    wts[n] = (WA, WB)

                def mmf(pb, n, m):
                    WA, WB = wts[n]
                    pairs = [(XT[:, j, m * 128:(m + 1) * 128], WA[:, j, :]) for j in range(8)]
                    pairs += [(XT[:, 8 + j, m * 128:(m + 1) * 128], WB[:, j, :]) for j in range(8)]
                    B.mm(pb, pairs)

                proj_phase(mmf, wload, x1_scr, lambda m, n: [("x1", m)], ALPHA, True, 0)
                g, b = load_gb(1)
                ln_pass(r_scr, x2_scr, g, b, "x2", True)

            load_x_to_XT(x, mem_prep)
            if not SKIP_FFN1:
                ffn(0, x, "xin", x1_scr, "x1", 0, True)
            final_src = None
            if STOP_AFTER == "ffn1":
                self.final_key = "x1"
                final_src = x1_scr
            elif STOP_AFTER == "ffn2x":
                load_x_to_XT(x1_scr) if False else None
                ffn(1, x1_scr, "x1", x2_scr, "x2", 2, False)
                self.final_key = "x2"
                final_src = x2_scr
            elif STOP_AFTER in ("m_prep", "m_Aproj", "m_A", "m_B", "m_M"):
                mixer(STOP_AFTER)
                oall = B.sb(ACTV, [128, 16, S], BF16)
                for a in range(16):
                    B.dma("pool", out[a * 128:(a + 1) * 128, :], oall[:, a, :], writes=[("out", a)])
            else:
                mixer()
                if STOP_AFTER == "mixer":
                    self.final_key = "x2"
                    final_src = x2_scr
                else:
                    ffn(1, x2_scr, "x2", out, "out", 2, False)

            if final_src is not None:
                for m in range(16):
                    R = B.sb(WORK + (m % 2) * 8192, [128, D], F32)
                    B.dma("sp", R, final_src[m * 128:(m + 1) * 128, :], reads=[(self.final_key, m)])
                    B.dma("sp", out[m * 128:(m + 1) * 128, :], R, writes=[("out", m)])
            B.P.add("sp", lambda h: h.nop(), reads=[("out", m) for m in range(16)], writes=[("done",)])

            esem = {"pe": s_pe, "act": s_act, "dve": s_dve, "pool": s_pool, "sp": s_sp}
            import contextlib
            with contextlib.ExitStack() as st:
                dsems = {}
                for q in ("pool", "sp", "act"):
                    for i in range(Prog.NDSEM):
                        if B.P.dma_count[q] > i:
                            dsems[(q, i)] = st.enter_context(nc.semaphore("d_%s_%d" % (q, i)))
                with nc.Block() as block:
                    handles = {}

                    @block.tensor
                    def _(h):
                        handles["pe"] = h
                        B.P.emit(handles, esem, dsems)("pe")

                    @block.scalar
                    def _(h):
                        handles["act"] = h
                        B.P.emit(handles, esem, dsems)("act")

                    @block.vector
                    def _(h):
                        handles["dve"] = h
                        B.P.emit(handles, esem, dsems)("dve")

                    @block.gpsimd
                    def _(h):
                        handles["pool"] = h
                        B.P.emit(handles, esem, dsems)("pool")

                    @block.sync
                    def _(h):
                        handles["sp"] = h
                        B.P.emit(handles, esem, dsems)("sp")
        return nc


def _rope_tables():
    t = np.arange(S)
    row = (t // 64).astype(np.float32)
    col = (t % 64).astype(np.float32)
    inv = (np.float32(10000.0) ** (-np.arange(0, 64, 2, dtype=np.float32) / np.float32(64))).astype(np.float32)
    ar = (row[:, None] * inv[None, :]).astype(np.float32)
    ac = (col[:, None] * inv[None, :]).astype(np.float32)
    cr, sr, cc, sc = np.cos(ar), np.sin(ar), np.cos(ac), np.sin(ac)
    C = np.concatenate([cr, cr, cc, cc], axis=1)
    Sg = np.concatenate([-sr, sr, -sc, sc], axis=1)
    return np.ascontiguousarray(np.concatenate([C, Sg], axis=1).astype(np.float32))


def _na_bias_layout(rpb):
    cols = np.arange(64)
    cs = np.clip(cols - 8, 0, 48)
    kr = np.arange(8)
    kc = np.arange(64)
    valid = (kc[:, None] >= cs[None, :]) & (kc[:, None] < cs[None, :] + 16)
    dc = np.clip(kc[:, None] - cols[None, :] + 15, 0, 30)
    outp = np.full((4, 8, 8, 64, 64), NEG, dtype=np.float32)
    for v in range(8):
        dr = kr - v + 7
        g = rpb[:, dr][:, :, dc]
        outp[:, v] = np.where(valid[None, None], g, np.float32(NEG))
    o = outp.reshape(4, 8, 4, 128, 64)
    o = o.transpose(0, 3, 1, 2, 4)
    return np.ascontiguousarray(o.reshape(4, 128, 8 * 4 * 64))


_CACHE = {}
INPUT_NAMES = ("x", "mem", "ln1_g", "ln1_b", "ffn1_w_gu", "ffn1_w_down", "w_in", "b_gate", "q_norm_a", "k_norm_a",
               "na_rpb", "w_mem_kv", "w_oa", "w_ob", "w_om", "w_out", "ln2_g", "ln2_b", "ffn2_w_gu", "ffn2_w_down",
               "ln3_g", "ln3_b")


def kernel(**inputs):
    f = lambda k: np.ascontiguousarray(np.asarray(inputs[k], dtype=np.float32))
    if "nc" not in _CACHE:
        _CACHE["nc"] = Builder().build()
    nc = _CACHE["nc"]
    x = f("x")
    mem = f("mem")
    shared = {}
    for i in (1, 2, 3):
        shared["ln%d_g" % i] = f("ln%d_g" % i).reshape(1, D)
        shared["ln%d_b" % i] = f("ln%d_b" % i).reshape(1, D)
    for i in (1, 2):
        shared["ffn%d_w_gu" % i] = f("ffn%d_w_gu" % i).reshape(D, 2 * DFF)
        shared["ffn%d_w_down" % i] = f("ffn%d_w_down" % i).reshape(DFF, D)
    shared["w_in"] = f("w_in").reshape(D, 9728)
    shared["b_gate"] = np.ascontiguousarray(f("b_gate").reshape(48, 128).T)
    qg = f("q_norm_a").reshape(128)
    kg = f("k_norm_a").reshape(128)
    shared["qk_gain"] = np.ascontiguousarray(np.concatenate([qg, qg, qg, qg, kg]).reshape(1, 640))
    shared["nab"] = _na_bias_layout(f("na_rpb").reshape(4, 15, 31))
    shared["w_mem_kv"] = f("w_mem_kv").reshape(D, 1024)
    shared["w_oa"] = f("w_oa").reshape(1024, D)
    shared["w_ob"] = f("w_ob").reshape(512, D)
    shared["w_om"] = f("w_om").reshape(512, D)
    shared["w_out"] = f("w_out").reshape(D, D)
    shared["ident"] = np.eye(128, dtype=np.float32)
    shared["rope"] = _rope_tables()
    in_maps = []
    for c in range(NCORES):
        m = dict(shared)
        m["x"] = np.ascontiguousarray(x[c])
        m["mem"] = np.ascontiguousarray(mem[c])
        in_maps.append(m)
    res = run_bass_kernel_spmd(nc, in_maps, core_ids=list(range(NCORES)))
    return np.stack([np.asarray(r["out"], dtype=np.float32) for r in res.results], axis=0)
```
